# Optimizing a Trainium2 kernel written in Bass

```python
import jax, jax.numpy as jnp
from jax import lax
import numpy as np


D_MODEL = 1024
BATCH = 8
SEQ = 2048
DEPTH = 4

GRID_W = 64
CTX_LEN = 256
N_HEADS = 8
QK_NOPE = 64
QK_ROPE = 32
V_DIM = 64
Q_LORA = 256
KV_LORA = 128
ROPE_BASE = 10000.0
Q_BLOCK = 128
SGU_DIM = 256
SGU_HEADS = 4
CHUNK = 128
POOL_DIM = 256
POOL_WINDOWS = (2, 4, 8, 16)
N_POOL = len(POOL_WINDOWS)
POOL_GDIM = POOL_DIM // N_POOL
FOURIER_DIM = 256
FOURIER_HEADS = 4
ATTN_DIM = N_HEADS * V_DIM
MIX_DIM = ATTN_DIM + SGU_DIM + POOL_DIM + FOURIER_DIM
OFF_Q = 0
OFF_KV = OFF_Q + Q_LORA
OFF_KR = OFF_KV + KV_LORA
OFF_SGU = OFF_KR + QK_ROPE
OFF_POOL = OFF_SGU + 2 * SGU_DIM
OFF_FOURIER = OFF_POOL + POOL_DIM
IN_DIM = OFF_FOURIER + FOURIER_DIM
D_FF = ((8 * D_MODEL + 3 * 256 - 1) // (3 * 256)) * 256
LN_EPS = 1e-6
DEEPNORM_ALPHA = (2.0 * DEPTH) ** 0.25
DEEPNORM_BETA = (8.0 * DEPTH) ** -0.25

kernel_name = 'hybrid_parallel_mixer_dit'


def layer_norm(x, g, b):
    xf = x.astype(jnp.float32)
    mu = jnp.mean(xf, axis=-1, keepdims=True)
    var = jnp.mean(jnp.square(xf - mu), axis=-1, keepdims=True)
    return ((xf - mu) * lax.rsqrt(var + LN_EPS)).astype(x.dtype) * g + b


def rms_norm(x, g):
    xf = x.astype(jnp.float32)
    return (xf * lax.rsqrt(jnp.mean(jnp.square(xf), axis=-1, keepdims=True) + LN_EPS)).astype(x.dtype) * g


def axial_rope_angles(length):
    rows = length // GRID_W
    row = jnp.repeat(jnp.arange(rows), GRID_W).astype(jnp.float32)
    col = jnp.tile(jnp.arange(GRID_W), rows).astype(jnp.float32)
    n_freq = QK_ROPE // 4
    inv_freq = ROPE_BASE ** (-jnp.arange(n_freq, dtype=jnp.float32) / n_freq)
    return row[:, None] * inv_freq, col[:, None] * inv_freq


def rotate(x, ang):
    k = x.shape[-1] // 2
    x1, x2 = x[..., :k], x[..., k:]
    cos = jnp.cos(ang)[None, :, None, :].astype(x.dtype)
    sin = jnp.sin(ang)[None, :, None, :].astype(x.dtype)
    return jnp.concatenate([x1 * cos - x2 * sin, x1 * sin + x2 * cos], axis=-1)


def axial_rope(x, angles):
    ang_r, ang_c = angles
    half = x.shape[-1] // 2
    return jnp.concatenate([rotate(x[..., :half], ang_r), rotate(x[..., half:], ang_c)], axis=-1)


def mla_qkv(proj, q_norm, w_uq, kv_norm, w_uk, w_uv, angles):
    B, L, _ = proj.shape
    cq = rms_norm(proj[..., OFF_Q:OFF_KV], q_norm)
    ckv = rms_norm(proj[..., OFF_KV:OFF_KR], kv_norm)
    k_rope = proj[..., OFF_KR:OFF_SGU][:, :, None, :]
    q = (cq @ w_uq).reshape(B, L, N_HEADS, QK_NOPE + QK_ROPE)
    k_nope = (ckv @ w_uk).reshape(B, L, N_HEADS, QK_NOPE)
    v = (ckv @ w_uv).reshape(B, L, N_HEADS, V_DIM)
    q_nope, q_rope = q[..., :QK_NOPE], q[..., QK_NOPE:]
    if angles is not None:
        q_rope = axial_rope(q_rope, angles)
        k_rope = axial_rope(k_rope, angles)
    q = jnp.concatenate([q_nope, q_rope], axis=-1)
    k = jnp.concatenate([k_nope, jnp.broadcast_to(k_rope, (B, L, N_HEADS, QK_ROPE))], axis=-1)
    return q, k, v


def attend(q, k, v):
    s = jnp.einsum('bqhd,bkhd->bhqk', q, k).astype(jnp.float32) * (QK_NOPE + QK_ROPE) ** -0.5
    p = jax.nn.softmax(s, axis=-1).astype(v.dtype)
    return jnp.einsum('bhqk,bkhd->bqhd', p, v)


def blocked_attend(q, k, v):
    B, L, H, d = q.shape
    nb = L // Q_BLOCK
    qb = q.reshape(B, nb, Q_BLOCK, H, d).transpose(1, 0, 2, 3, 4)
    ob = lax.map(lambda qi: attend(qi, k, v), qb)
    return ob.transpose(1, 0, 2, 3, 4).reshape(B, L, H * V_DIM)


def spatial_gating(uv, g, b, w_s, b_s):
    B, L, _ = uv.shape
    u, v = uv[..., :SGU_DIM], uv[..., SGU_DIM:]
    v = layer_norm(v, g, b).reshape(B, L // CHUNK, CHUNK, SGU_HEADS, SGU_DIM // SGU_HEADS)
    mixed = jnp.einsum('gpq,bnqgc->bnpgc', w_s, v) + b_s.T[:, :, None]
    return u * mixed.reshape(B, L, SGU_DIM)


def multiscale_pool(p, w_pool, pool_scale):
    B, L, _ = p.shape
    cs = jnp.concatenate([jnp.zeros((B, 1, POOL_DIM), jnp.float32),
                          jnp.cumsum(p.astype(jnp.float32), axis=1)], axis=1)
    t = jnp.arange(L)
    outs = []
    for gi, w in enumerate(POOL_WINDOWS):
        lo = jnp.clip(t - w // 2, 0, L)
        hi = jnp.clip(t + w // 2, 0, L)
        seg = cs[:, :, gi * POOL_GDIM:(gi + 1) * POOL_GDIM]
        mean = (seg[:, hi] - seg[:, lo]) / (hi - lo).astype(jnp.float32)[None, :, None]
        tok = p[..., gi * POOL_GDIM:(gi + 1) * POOL_GDIM]
        outs.append((mean.astype(p.dtype) - tok) @ w_pool[gi])
    return jnp.concatenate(outs, axis=-1) * pool_scale


def fourier_mix(f, w_f):
    B, L, _ = f.shape
    fh = f.astype(jnp.float32).reshape(B, L, FOURIER_HEADS, FOURIER_DIM // FOURIER_HEADS)
    spec = jnp.fft.fft2(fh, axes=(1, 3), norm='ortho').real
    return spec.astype(f.dtype).reshape(B, L, FOURIER_DIM) @ w_f


def local_mixers(proj, sgu_g, sgu_b, w_s, b_s, w_pool, pool_scale, w_f):
    return jnp.concatenate([
        spatial_gating(proj[..., OFF_SGU:OFF_POOL], sgu_g, sgu_b, w_s, b_s),
        multiscale_pool(proj[..., OFF_POOL:OFF_FOURIER], w_pool, pool_scale),
        fourier_mix(proj[..., OFF_FOURIER:IN_DIM], w_f),
    ], axis=-1)


def residual_tail(x, mix, g1, sh2, sc2, g2, w_out, ln1g, ln1b, w1, w3, w2, ln2g, ln2b):
    x = layer_norm(DEEPNORM_ALPHA * x + g1 * (mix @ w_out), ln1g, ln1b)
    h = x * (1.0 + sc2) + sh2
    ffn = (jax.nn.silu(h @ w1) * (h @ w3)) @ w2
    return layer_norm(DEEPNORM_ALPHA * x + g2 * ffn, ln2g, ln2b)


def setup_inputs(seed: int = 0) -> dict:
    key = jax.random.key(seed)
    ks = jax.random.split(key, 32)
    f32 = jnp.float32

    def nrm(k, shape, scale):
        return jax.random.normal(k, shape, f32) * scale

    def gain(k, shape):
        return 1.0 + 0.02 * jax.random.normal(k, shape, f32)

    beta = DEEPNORM_BETA
    return {
        'x': nrm(ks[0], (BATCH, SEQ, D_MODEL), 1.0),
        'c': nrm(ks[1], (BATCH, D_MODEL), 1.0),
        'ctx': nrm(ks[2], (BATCH, CTX_LEN, D_MODEL), 1.0),
        'c_ctx': nrm(ks[3], (D_MODEL,), 1.0),
        'w_mod': nrm(ks[4], (DEPTH, D_MODEL, 6 * D_MODEL), 0.5 * D_MODEL ** -0.5),
        'b_mod': nrm(ks[5], (DEPTH, 6 * D_MODEL), 0.02),
        'w_in': nrm(ks[6], (DEPTH, D_MODEL, IN_DIM), D_MODEL ** -0.5),
        'q_norm': gain(ks[7], (DEPTH, Q_LORA)),
        'w_uq': nrm(ks[8], (DEPTH, Q_LORA, N_HEADS * (QK_NOPE + QK_ROPE)), Q_LORA ** -0.5),
        'kv_norm': gain(ks[9], (DEPTH, KV_LORA)),
        'w_uk': nrm(ks[10], (DEPTH, KV_LORA, N_HEADS * QK_NOPE), KV_LORA ** -0.5),
        'w_uv': nrm(ks[11], (DEPTH, KV_LORA, N_HEADS * V_DIM), KV_LORA ** -0.5),
        'sgu_ln_g': gain(ks[12], (DEPTH, SGU_DIM)),
        'sgu_ln_b': nrm(ks[13], (DEPTH, SGU_DIM), 0.02),
        'w_spatial': nrm(ks[14], (DEPTH, SGU_HEADS, CHUNK, CHUNK), CHUNK ** -0.5),
        'b_spatial': gain(ks[15], (DEPTH, SGU_HEADS, CHUNK)),
        'w_pool': nrm(ks[16], (DEPTH, N_POOL, POOL_GDIM, POOL_GDIM), POOL_GDIM ** -0.5),
        'pool_scale': gain(ks[17], (DEPTH, POOL_DIM)),
        'w_fourier': nrm(ks[18], (DEPTH, FOURIER_DIM, FOURIER_DIM), FOURIER_DIM ** -0.5),
        'w_out': nrm(ks[19], (DEPTH, MIX_DIM, D_MODEL), beta * MIX_DIM ** -0.5),
        'ln1_g': gain(ks[20], (DEPTH, D_MODEL)),
        'ln1_b': nrm(ks[21], (DEPTH, D_MODEL), 0.02),
        'w_ffn1': nrm(ks[22], (DEPTH, D_MODEL, D_FF), D_MODEL ** -0.5),
        'w_ffn3': nrm(ks[23], (DEPTH, D_MODEL, D_FF), D_MODEL ** -0.5),
        'w_ffn2': nrm(ks[24], (DEPTH, D_FF, D_MODEL), beta * D_FF ** -0.5),
        'ln2_g': gain(ks[25], (DEPTH, D_MODEL)),
        'ln2_b': nrm(ks[26], (DEPTH, D_MODEL), 0.02),
    }


def reference(x, c, ctx, c_ctx, w_mod, b_mod, w_in, q_norm, w_uq, kv_norm, w_uk, w_uv,
              sgu_ln_g, sgu_ln_b, w_spatial, b_spatial, w_pool, pool_scale, w_fourier,
              w_out, ln1_g, ln1_b, w_ffn1, w_ffn3, w_ffn2, ln2_g, ln2_b):
    B, L, _ = x.shape
    angles = axial_rope_angles(L)
    silu_c = jax.nn.silu(c)
    silu_cc = jax.nn.silu(c_ctx)
    x_ctx = ctx
    for l in range(DEPTH):
        last = l == DEPTH - 1
        mod = silu_c @ w_mod[l] + b_mod[l]
        sh1, sc1, g1, sh2, sc2, g2 = [m[:, None, :] for m in jnp.split(mod, 6, axis=-1)]
        mod_c = silu_cc @ w_mod[l] + b_mod[l]
        sh1c, sc1c, g1c, sh2c, sc2c, g2c = jnp.split(mod_c, 6, axis=-1)

        h = x * (1.0 + sc1) + sh1
        h_c = x_ctx * (1.0 + sc1c) + sh1c
        proj = h @ w_in[l]
        proj_c = h_c @ w_in[l]

        q, k, v = mla_qkv(proj, q_norm[l], w_uq[l], kv_norm[l], w_uk[l], w_uv[l], angles)
        q_c, k_c, v_c = mla_qkv(proj_c, q_norm[l], w_uq[l], kv_norm[l], w_uk[l], w_uv[l], None)

        attn = blocked_attend(q, jnp.concatenate([k_c, k], axis=1), jnp.concatenate([v_c, v], axis=1))
        mix = jnp.concatenate([attn, local_mixers(proj, sgu_ln_g[l], sgu_ln_b[l], w_spatial[l], b_spatial[l],
                                                  w_pool[l], pool_scale[l], w_fourier[l])], axis=-1)
        x_new = residual_tail(x, mix, g1, sh2, sc2, g2, w_out[l], ln1_g[l], ln1_b[l],
                              w_ffn1[l], w_ffn3[l], w_ffn2[l], ln2_g[l], ln2_b[l])

        if not last:
            attn_c = attend(q_c, k_c, v_c).reshape(B, x_ctx.shape[1], ATTN_DIM)
            mix_c = jnp.concatenate([attn_c, local_mixers(proj_c, sgu_ln_g[l], sgu_ln_b[l], w_spatial[l],
                                                          b_spatial[l], w_pool[l], pool_scale[l],
                                                          w_fourier[l])], axis=-1)
            x_ctx = residual_tail(x_ctx, mix_c, g1c, sh2c, sc2c, g2c, w_out[l], ln1_g[l], ln1_b[l],
                                  w_ffn1[l], w_ffn3[l], w_ffn2[l], ln2_g[l], ln2_b[l])
        x = x_new
    return x
```

```python
import contextlib
import math
import numpy as np
import concourse.bass as bass
import concourse.mybir as mybir
from concourse.bass_utils import run_bass_kernel_spmd

F32 = mybir.dt.float32
BF16 = mybir.dt.bfloat16
AF = mybir.ActivationFunctionType
ALU = mybir.AluOpType

D = 1024
DEPTH = 4
SEQ = 2048
CTX = 256
NTOK = SEQ + CTX
NT = NTOK // 128
IN_DIM = 1440
D_FF = 2816
MIX = 1280
ALPHA = (2.0 * DEPTH) ** 0.25
EPS = 1e-6
TP = 8 + CTX + 8 + SEQ + 8
BLOCKS = [(0, 256), (256, 512), (768, 512), (1280, 512), (1792, 512)]
ROPE_PERM = list(range(8, 16)) + list(range(0, 8)) + list(range(24, 32)) + list(range(16, 24))


def pcol(tok):
    return tok + 8 if tok < CTX else tok + 16


class _Buf:
    __slots__ = ("name", "w", "readers", "dsem", "dcount", "excl")

    def __init__(self, name, excl=False):
        self.name = name
        self.excl = excl
        self.w = None
        self.readers = {}
        self.dsem = None
        self.dcount = 0


class Sched:
    ENGS = ("pe", "act", "dve", "pool", "sp")

    def __init__(self, nc, stack):
        self.nc = nc
        self.stack = stack
        self.sem, self.cnt, self.prog, self.seen = {}, {}, {}, {}
        for e in self.ENGS:
            self.sem[e] = stack.enter_context(nc.semaphore("s_" + e))
            self.cnt[e] = 0
            self.prog[e] = []
            self.seen[e] = {}
        self.final_bufs = []
        self.dma_bufs = {}
        self.nsem = 5
        self.engobj = {"pe": nc.tensor, "act": nc.scalar, "dve": nc.vector, "pool": nc.gpsimd, "sp": nc.sync}

    def sbuf(self, name, shape, dt):
        return self.stack.enter_context(self.nc.sbuf_tensor(name, list(shape), dt))

    def psum(self, name, shape, dt):
        return self.stack.enter_context(self.nc.psum_tensor(name, list(shape), dt))

    def _need(self, eng, dep, waits):
        if dep is None:
            return
        if dep[0] == "eng":
            key, val, sem = ("eng", dep[1]), dep[2], self.sem[dep[1]]
        else:
            key, val, sem = ("dma", id(dep[1])), dep[2], dep[1].dsem
        if self.seen[eng].get(key, 0) >= val:
            return
        self.seen[eng][key] = val
        waits.append((sem, val))

    def _deps(self, eng, reads, writes, is_dma=False):
        waits = []
        for b in reads:
            if b.w is not None:
                self._need(eng, b.w, waits)
            if b.excl:
                for k, r in b.readers.items():
                    if r[0] == "eng" and r[1] == eng:
                        continue
                    self._need(eng, r, waits)
        for b in writes:
            d = b.w
            if d is not None:
                if d[0] == "eng" and d[1] == eng and not is_dma and eng == "pe":
                    pass
                elif d[0] == "dma" and is_dma and not b.readers:
                    pass
                else:
                    self._need(eng, d, waits)
            for k, r in b.readers.items():
                if r[0] == "eng" and r[1] == eng and not is_dma and eng == "pe":
                    continue
                self._need(eng, r, waits)
        return waits

    def op(self, eng, fn, reads=(), writes=()):
        waits = self._deps(eng, reads, writes)
        self.cnt[eng] += 1
        me = ("eng", eng, self.cnt[eng])
        for b in reads:
            b.readers[eng] = me
        for b in writes:
            b.w = me
            b.readers = {}
        e = self.engobj[eng]
        for sem, val in waits:
            e.wait_ge(sem, val)
        fn(e).then_inc(self.sem[eng], 1)

    def dma(self, queue, out_ap, in_ap, reads=(), writes=(), final=False):
        waits = self._deps(queue, reads, writes, is_dma=True)
        b = writes[0]
        if b.dsem is None:
            b.dsem = self.stack.enter_context(self.nc.semaphore("d_" + b.name))
            self.nsem += 1
        b.dcount += 16
        me = ("dma", b, b.dcount)
        for r in reads:
            r.readers[("dma", id(b))] = me
        b.w = me
        b.readers = {}
        self.dma_bufs[id(b)] = b
        if final:
            self.final_bufs.append(b)

        e = self.engobj[queue]
        for sem, val in waits:
            e.wait_ge(sem, val)
        e.dma_start(out=out_ap, in_=in_ap).then_inc(b.dsem, 16)

    def barrier(self):
        snap = dict(self.cnt)
        for e in self.ENGS:
            waits = []
            for f in self.ENGS:
                if f != e and snap[f] > 0:
                    self._need(e, ("eng", f, snap[f]), waits)
            for b in self.dma_bufs.values():
                self._need(e, ("dma", b, b.dcount), waits)
            for sem, val in waits:
                self.engobj[e].wait_ge(sem, val)

    def emit(self):
        waits = []
        for b in self.final_bufs:
            self._need("sp", b.w, waits)
        for sem, val in waits:
            self.engobj["sp"].wait_ge(sem, val)


class Ring:
    def __init__(self, items):
        self.items = items
        self.i = 0

    def next(self):
        it = self.items[self.i % len(self.items)]
        self.i += 1
        return it


def build(depth=DEPTH, taps=(), stop=None):
    nc = bass.Bass("TRN2", target_bir_lowering=False)
    dr = {}
    _reg = {}

    def Buf(name, excl=False):
        if name not in _reg:
            _reg[name] = _Buf(name, excl)
        return _reg[name]

    def din(name, shape):
        dr[name] = nc.dram_tensor(name, list(shape), F32, kind="ExternalInput").ap()
        return dr[name]

    x_in = din("x_in", [NTOK, D])
    cvec = din("cvec", [128, 8, 2])
    w_mod = din("w_mod", [DEPTH, D, 6 * D])
    b_modT = din("b_modT", [128, DEPTH * 48])
    b_mod = din("b_mod", [DEPTH, 6 * D])
    w_in = din("w_in", [DEPTH, D, IN_DIM])
    w_in_krp = din("w_in_krp", [DEPTH, D, 32])
    q_normT = din("q_normT", [DEPTH, 128, 2])
    kv_normT = din("kv_normT", [DEPTH, 128, 1])
    w_uq = din("w_uq", [DEPTH, 256, 768])
    w_uq_rp = din("w_uq_rp", [DEPTH, 256, 256])
    w_uk = din("w_uk", [DEPTH, 128, 512])
    w_uv = din("w_uv", [DEPTH, 128, 512])
    sgu_g = din("sgu_g", [DEPTH, 256])
    sgu_b = din("sgu_b", [DEPTH, 256])
    w_spT = din("w_spT", [DEPTH, 4, 128, 128])
    b_sp = din("b_sp", [DEPTH, 512])
    w_pool = din("w_pool", [DEPTH, 4, 64, 64])
    pool_scT = din("pool_scT", [DEPTH, 128, 2])
    w_fou = din("w_fou", [DEPTH, 256, 256])
    w_out = din("w_out", [DEPTH, MIX, D])
    ln1_g = din("ln1_g", [DEPTH, D])
    ln1_b = din("ln1_b", [DEPTH, D])
    w_f1 = din("w_f1", [DEPTH, D, D_FF])
    w_f3 = din("w_f3", [DEPTH, D, D_FF])
    w_f2 = din("w_f2", [DEPTH, D_FF, D])
    ln2_g = din("ln2_g", [DEPTH, D])
    ln2_b = din("ln2_b", [DEPTH, D])
    k_ident = din("k_ident", [128, 128])
    k_bdc = din("k_bdc", [128, 128])
    k_bds = din("k_bds", [128, 128])
    k_cos = din("k_cos", [128, SEQ])
    k_sin = din("k_sin", [128, SEQ])
    k_c256 = din("k_c256", [256, 256])
    k_s256 = din("k_s256", [256, 256])
    k_cL = din("k_cL", [SEQ, SEQ])
    k_sL = din("k_sL", [SEQ, SEQ])
    k_edge = din("k_edge", [128, 64])
    out_d = nc.dram_tensor("out", [SEQ, D], F32, kind="ExternalOutput").ap()
    gsc = nc.dram_tensor("gscratch", [DEPTH * 4, D], F32, kind="Internal").ap()
    tap_d = {}
    for nm, shp in taps:
        tap_d[nm] = nc.dram_tensor("tap_" + nm, list(shp), F32, kind="ExternalOutput").ap()

    with contextlib.ExitStack() as st:
        S = Sched(nc, st)
        op, dma = S.op, S.dma

        XS = S.sbuf("XS", [128, NT, D], F32)
        bXS = [Buf("xs%d" % j) for j in range(NT)]
        R = S.sbuf("R", [128, 10, NTOK], BF16)
        ident = S.sbuf("ident", [128, 128], BF16)
        BDC = S.sbuf("BDC", [128, 128], BF16)
        BDS = S.sbuf("BDS", [128, 128], BF16)
        COS = S.sbuf("COS", [128, SEQ], BF16)
        SIN = S.sbuf("SIN", [128, SEQ], BF16)
        C256 = S.sbuf("C256", [128, 2, 256], BF16)
        S256 = S.sbuf("S256", [128, 2, 256], BF16)
        EDGE = S.sbuf("EDGE", [128, 64], F32)
        MODT = S.sbuf("MODT", [128, DEPTH * 48, 2], F32)
        OPSC = S.sbuf("OPSC", [128, DEPTH * 48, 2], F32)
        SILC = S.sbuf("SILC", [128, 8, 2], BF16)
        BMTp = S.sbuf("BMTp", [128, DEPTH * 48], F32)
        VEC = S.sbuf("VEC", [128, 8], F32)
        ST6 = S.sbuf("ST6", [128, 2, 6], F32)
        MV = S.sbuf("MV", [128, 2], F32)
        RS = S.sbuf("RS", [128, 1], F32)
        LST = S.sbuf("LST", [128, 4, 2, 6], F32)
        LMV = S.sbuf("LMV", [128, 4, 4], F32)
        lnring = Ring([(i, _Buf("lnst%d" % i)) for i in range(4)])
        EPST = S.sbuf("EPST", [128, 1], F32)
        ONES = S.sbuf("ONES", [128, 128], BF16)
        bConst = Buf("const")
        bMod = Buf("mod")
        bVec = Buf("vec")
        bSt = Buf("st")

        used = 0
        arena_elems = int(nc.sbuf_bytes_remaining) // 2 - 64
        AR = S.sbuf("ARENA", [128, arena_elems], BF16)
        ARB = arena_elems * 2

        def aview(off, shape, dt=BF16):
            n = int(np.prod(shape))
            nb = n * (2 if dt == BF16 else 4)
            assert off % 4 == 0 and off + nb <= ARB, ("arena overflow", off, nb, ARB)
            v = AR[:, off // 2: off // 2 + nb // 2]
            if dt == F32:
                v = v.bitcast(F32)
            if len(shape) == 2:
                v = v.rearrange("p (a b) -> p a b", a=shape[0])
            elif len(shape) == 3:
                v = v.rearrange("p (a b c) -> p a b c", a=shape[0], b=shape[1])
            return v

        def rview(c0, nchunk, shape, dt=BF16):
            n = int(np.prod(shape))
            nb = n * (2 if dt == BF16 else 4)
            assert nb <= nchunk * NTOK * 2
            v = R[:, c0:c0 + nchunk, :].rearrange("p a b -> p (a b)")[:, 0: nb // 2]
            if dt == F32:
                v = v.bitcast(F32)
            if len(shape) == 2:
                v = v.rearrange("p (a b) -> p a b", a=shape[0])
            elif len(shape) == 3:
                v = v.rearrange("p (a b c) -> p a b c", a=shape[0], b=shape[1])
            return v

        PW = [S.psum("PW%d" % i, [128, 1024], F32) for i in range(4)]
        PB7 = PW[3][:, 512:1024].bitcast(BF16)
        bPS = [Buf("ps%d" % i, excl=True) for i in range(8)]

        def bank(i):
            return PW[i // 2][:, (i % 2) * 512:(i % 2) * 512 + 512]

        dma("pool", ident[:], k_ident[:], writes=[bConst])
        dma("pool", BDC[:], k_bdc[:], writes=[bConst])
        dma("pool", BDS[:], k_bds[:], writes=[bConst])
        dma("pool", COS[:], k_cos[:], writes=[bConst])
        dma("pool", SIN[:], k_sin[:], writes=[bConst])
        dma("pool", C256[:], k_c256.rearrange("(k p) n -> p k n", p=128), writes=[bConst])
        dma("pool", S256[:], k_s256.rearrange("(k p) n -> p k n", p=128), writes=[bConst])
        dma("pool", EDGE[:], k_edge[:], writes=[bConst])
        op("dve", lambda e: e.memset(EPST[:], EPS), writes=[bConst])
        op("dve", lambda e: e.memset(ONES[:], 1.0), writes=[bConst])
        for j in range(NT):
            dma("sp", XS[:, j, :], x_in[j * 128:(j + 1) * 128, :], writes=[bXS[j]])

        if stop == "p0a":
            S.barrier(); S.emit()
            return nc
        CV = aview(0, [8, 2], F32)
        bCV = Buf("cv")
        dma("sp", CV, cvec[:], writes=[bCV])
        dma("sp", BMTp[:], b_modT[:], writes=[bCV])
        op("act", lambda e: e.activation(out=SILC[:], in_=CV, func=AF.Silu), reads=[bCV], writes=[bMod])
        bG = Buf("gsc")

        def mod_layer(l, ring, colbank, rowbanks, BROW, GROW, bBR, bGR):
            nring = len(ring.items)
            slots = {}

            def issue(pc):
                slots[pc] = ring.next()
                dma("pool", slots[pc][0], w_mod[l][:, pc * 512:(pc + 1) * 512].rearrange("(k p) n -> p k n", p=128),
                    writes=[slots[pc][1]])
            for pc in range(min(nring - 1, 12)):
                issue(pc)
            yield
            for pc in range(12):
                if pc + nring - 1 < 12:
                    issue(pc + nring - 1)
                wt, wb = slots.pop(pc)

                def mm(e):
                    ins = None
                    for mc in range(4):
                        col = (pc * 4 + mc) * 2
                        for k in range(8):
                            ins = e.matmul(bank(colbank)[:, col:col + 2], lhsT=wt[:, k, mc * 128:(mc + 1) * 128],
                                           rhs=SILC[:, k, :], start=(k == 0), stop=(k == 7))
                    return ins
                op("pe", mm, reads=[wb, bMod], writes=[bPS[colbank]])
                if pc in (4, 5, 10, 11):
                    gi, half = (0, pc - 4) if pc < 10 else (1, pc - 10)
                    rowbank = rowbanks[pc % 2]

                    def rowmm(e):
                        ins = None
                        for k in range(8):
                            ins = e.matmul(bank(rowbank)[0:2, 0:512], lhsT=SILC[:, k, :], rhs=wt[:, k, :],
                                           start=(k == 0), stop=(k == 7))
                        return ins
                    op("pe", rowmm, reads=[wb, bMod], writes=[bPS[rowbank]])
                    dma("sp", BROW[0:2, :], b_mod[l, pc * 512:(pc + 1) * 512].partition_broadcast(2), writes=[bBR])
                    op("dve", lambda e: e.tensor_tensor(out=GROW[0:2, :], in0=bank(rowbank)[0:2, 0:512], in1=BROW[0:2, :],
                                                        op=ALU.add), reads=[bPS[rowbank], bBR], writes=[bGR])
                    for w in range(2):
                        row = l * 4 + gi * 2 + w
                        dma("sp", gsc[row:row + 1, half * 512:(half + 1) * 512], GROW[w:w + 1, :], reads=[bGR], writes=[bG])
                yield
            for w in range(2):
                op("dve", lambda e: e.tensor_tensor(
                    out=MODT[:, l * 48:(l + 1) * 48, w],
                    in0=bank(colbank)[:, 0:96].rearrange("p (a b) -> p a b", b=2)[:, :, w],
                    in1=BMTp[:, l * 48:(l + 1) * 48], op=ALU.add), reads=[bPS[colbank], bCV], writes=[bMod])
            op("dve", lambda e: e.tensor_scalar_add(out=OPSC[:, l * 48:(l + 1) * 48, :], in0=MODT[:, l * 48:(l + 1) * 48, :],
                                                    scalar1=1.0), reads=[bMod], writes=[bMod])
            yield

        WM0 = Ring([(aview(4096 + i * 8192, [8, 512]), Buf("wm%d" % i)) for i in range(4)])
        for _ in mod_layer(0, WM0, 0, (1, 2), aview(36864, [512], F32), aview(38912, [512], F32), Buf("brow"), Buf("grow")):
            pass
        S.barrier()

        if stop == "p0":
            S.emit()
            return nc

        def modcol(l, part, c, w):
            i = l * 48 + part * 8 + c
            return MODT[:, i, w:w + 1]

        def opscol(l, part, c, w):
            i = l * 48 + part * 8 + c
            return OPSC[:, i, w:w + 1]

        def ln_a(j):
            q, bq = lnring.next()
            st, mv = LST[:, q, :, :], LMV[:, q, :]
            op("dve", lambda e: e.bn_stats(out=st[:, 0, :], in_=XS[:, j, 0:512]), reads=[bXS[j]], writes=[bq])
            op("dve", lambda e: e.bn_stats(out=st[:, 1, :], in_=XS[:, j, 512:1024]), reads=[bXS[j]], writes=[bq])
            op("dve", lambda e: e.bn_aggr(out=mv[:, 0:2], in_=st), reads=[bq], writes=[bq])
            op("act", lambda e: e.activation(out=mv[:, 2:3], in_=mv[:, 1:2], func=AF.Sqrt, bias=EPST[:], scale=1.0),
               reads=[bq, bConst], writes=[bq])
            return (j, mv, bq)

        def ln_b(c):
            j, mv, bq = c
            xt = XS[:, j, :]
            op("dve", lambda e: e.reciprocal(out=mv[:, 2:3], in_=mv[:, 2:3]), reads=[bq], writes=[bq])
            op("dve", lambda e: e.scalar_tensor_tensor(out=mv[:, 3:4], in0=mv[:, 0:1], scalar=-1.0, in1=mv[:, 2:3],
                                                       op0=ALU.mult, op1=ALU.mult), reads=[bq], writes=[bq])
            op("act", lambda e: e.activation(out=xt, in_=xt, func=AF.Identity, bias=mv[:, 3:4], scale=mv[:, 2:3]),
               reads=[bq, bXS[j]], writes=[bXS[j]])
            return c

        def ln_c(c, gbc, bbc, bGB):
            j = c[0]
            xt = XS[:, j, :]
            op("dve", lambda e: e.tensor_tensor(out=xt, in0=xt, in1=gbc, op=ALU.mult), reads=[bXS[j], bGB], writes=[bXS[j]])
            op("dve", lambda e: e.tensor_tensor(out=xt, in0=xt, in1=bbc, op=ALU.add), reads=[bXS[j], bGB], writes=[bXS[j]])

        def layer_norm_tile(j, gbc, bbc, bGB):
            ln_c(ln_b(ln_a(j)), gbc, bbc, bGB)

        def make_hT(j, X16, bX16, dst, bdsts, l, part_sc, part_sh, tb):
            w = 1 if j < 2 else 0
            pvs = [bank(tb[0]).bitcast(BF16), bank(tb[1]).bitcast(BF16)]
            op("act", lambda e: e.activation(out=X16, in_=XS[:, j, :], func=AF.Copy), reads=[bXS[j]], writes=[bX16])

            def tr(e):
                ins = None
                for c in range(8):
                    ins = e.transpose(pvs[c // 4][:, (c % 4) * 128:(c % 4 + 1) * 128], X16[:, c * 128:(c + 1) * 128], ident[:])
                return ins
            op("pe", tr, reads=[bX16, bConst], writes=[bPS[tb[0]], bPS[tb[1]]])
            for c in range(4):
                op("act", lambda e: e.activation(out=dst[:, c, :], in_=pvs[0][:, c * 128:(c + 1) * 128],
                                                 func=AF.Identity, bias=modcol(l, part_sh, c, w),
                                                 scale=opscol(l, part_sc, c, w)),
                   reads=[bPS[tb[0]], bMod], writes=[bdsts[0]])
                c2 = c + 4
                op("dve", lambda e: e.tensor_scalar(out=dst[:, c2, :], in0=pvs[1][:, c * 128:(c + 1) * 128],
                                                    scalar1=opscol(l, part_sc, c2, w),
                                                    scalar2=modcol(l, part_sh, c2, w),
                                                    op0=ALU.mult, op1=ALU.add),
                   reads=[bPS[tb[1]], bMod], writes=[bdsts[1]])

        def tap(name, src_ap, reads):
            if name in tap_d:
                dma("pool", tap_d[name], src_ap, reads=reads, writes=[Buf("tap_" + name)], final=True)

        CQT = aview(0, [2, NTOK]); bCQ = [Buf("cq%d" % i) for i in range(5)]
        CKVT = aview(9216, [NTOK]); bCKV = [Buf("ckv%d" % i) for i in range(5)]
        KRT = aview(13824, [NTOK]); bKR = [Buf("kr%d" % i) for i in range(5)]
        PT = aview(18432, [2, TP]); bPT = Buf("pT")
        FTOK = aview(27776, [NT, 256]); bFT = [Buf("ft%d" % j) for j in range(NT)]
        bMIX = [Buf("mix%d" % i) for i in range(5)]
        op("pool", lambda e: e.memset(PT, 0.0), writes=[bPT])
        S.barrier()

        for l in range(depth):
            last = (l == DEPTH - 1)
            WIN = aview(36992, [8, IN_DIM]); bWIN = Buf("win")
            WKB = aview(60032, [64 + 256]); bWKB = Buf("wkb")
            WST = aview(60672, [4, 128]); bWST = Buf("wst")
            WPB = aview(61696, [2, 128]); bWPB = Buf("wpb")
            SGB = aview(62208, [2, 256], F32); bSGB = Buf("sgb")
            BSB = aview(64256, [4, 128], F32)
            _r0 = rview(0, 4, [9216])
            _r0f = rview(0, 4, [4608], F32)
            _b16 = rview(6, 2, [4096])
            _b32 = rview(6, 2, [2048], F32)
            H1Ts = [(rview(8, 2, [8, 512]), (Buf("h1t0a"), Buf("h1t0d"))),
                    (_r0[:, 0:4096].rearrange("p (a b) -> p a b", a=8), (Buf("h1t1a"), Buf("h1t1d")))]
            X16s = Ring([(_b16[:, 0:1024], Buf("x16a")), (_r0[:, 4096:5120], Buf("x16b_"))])
            UTs = [(aview(66304, [2, 512]), Buf("ut0")), (_r0[:, 5120:6144].rearrange("p (a b) -> p a b", a=2), Buf("ut1"))]
            VNs = Ring([(aview(68352, [256], F32), aview(69376, [256]), aview(69888, [2, 128], F32), Buf("vn0")),
                        (_r0f[:, 3072:3328], _r0[:, 6656:6912],
                         _r0f[:, 3456:3712].rearrange("p (a b) -> p a b", a=2), Buf("vn1"))])
            TMPA = _b32[:, 512:1024]
            TMPB = _b32[:, 1024:1536]
            SQ = _b16[:, 3072:4096].rearrange("p (a b) -> p a b", a=2)
            bTA, bTB, bSQ = Buf("tmpa"), Buf("tmpb"), Buf("sq")
            RAWQ = aview(70912, [2, 512], F32); bRAWQ = Buf("rawq")
            RINVQ = aview(75008, [512], F32); bRINVQ = Buf("rinvq")
            tm_banks = Ring([3, 5])

            dma("pool", WIN, w_in[l].rearrange("(k p) n -> p k n", p=128), writes=[bWIN])
            op("dve", lambda e: e.memset(PT, 0.0), writes=[bPT])
            op("dve", lambda e: e.memset(WKB, 0.0), writes=[bWKB])
            dma("pool", WKB[:, 64:320].rearrange("p (k n) -> p k n", k=8),
                w_in_krp[l].rearrange("(k p) n -> p k n", p=128), writes=[bWKB])
            dma("pool", WST, w_spT[l].rearrange("g q p -> q g p"), writes=[bWST])
            op("dve", lambda e: e.memset(WPB, 0.0), writes=[bWPB])
            for g in range(4):
                r0 = (g % 2) * 64
                dma("pool", WPB[r0:r0 + 64, g // 2, r0:r0 + 64], w_pool[l, g], writes=[bWPB])
            dma("sp", SGB[:, 0, :], sgu_g[l, :].partition_broadcast(128), writes=[bSGB])
            dma("sp", SGB[:, 1, :], sgu_b[l, :].partition_broadcast(128), writes=[bSGB])
            dma("sp", BSB.rearrange("p a b -> p (a b)"), b_sp[l, :].partition_broadcast(128), writes=[bSGB])
            dma("sp", VEC[:, 0:2], q_normT[l], writes=[bVec])
            dma("sp", VEC[:, 2:3], kv_normT[l], writes=[bVec])
            dma("sp", VEC[:, 3:5], pool_scT[l], writes=[bVec])

            fm_banks = Ring([0, 1, 2])

            def stageA(bi):
                bs, bn = BLOCKS[bi]
                H1T, bH1 = H1Ts[bi % 2]
                for jj in range(bn // 128):
                    X16, bX16 = X16s.next()
                    make_hT(bs // 128 + jj, X16, bX16, H1T[:, :, jj * 128:(jj + 1) * 128], bH1, l, 1, 0, (7, 4))
                    yield

            def stageB(bi):
                bs, bn = BLOCKS[bi]
                H1T, bH1 = H1Ts[bi % 2]
                UT, bUT = UTs[bi % 2]

                def fm_mm(e, pb, lhs_fn, m):
                    ins = None
                    for k in range(8):
                        ins = e.matmul(bank(pb)[0:m, 0:bn], lhsT=lhs_fn(k), rhs=H1T[:, k, 0:bn],
                                       start=(k == 0), stop=(k == 7))
                    return ins
                for ci in range(2):
                    pb = fm_banks.next()
                    c0 = 416 + ci * 128
                    op("pe", lambda e: fm_mm(e, pb, lambda k: WIN[:, k, c0:c0 + 128], 128),
                       reads=[bWIN, *bH1], writes=[bPS[pb]])
                    op("act", lambda e: e.activation(out=UT[:, ci, 0:bn], in_=bank(pb)[:, 0:bn], func=AF.Copy),
                       reads=[bPS[pb]], writes=[bUT])
                    yield
                for grp, cols, nrm in (("q", (0, 128), 256.0), ("kv", (256,), 128.0)):
                    pbs = []
                    for ci, c0 in enumerate(cols):
                        pb = fm_banks.next()
                        pbs.append(pb)
                        op("pe", lambda e: fm_mm(e, pb, lambda k: WIN[:, k, c0:c0 + 128], 128),
                           reads=[bWIN, *bH1], writes=[bPS[pb]])
                        op("act", lambda e: e.activation(out=SQ[:, ci, 0:bn], in_=bank(pb)[:, 0:bn], func=AF.Square),
                           reads=[bPS[pb]], writes=[bSQ])
                        if grp == "q":
                            op("act", lambda e: e.activation(out=RAWQ[:, ci, 0:bn], in_=bank(pb)[:, 0:bn], func=AF.Copy),
                               reads=[bPS[pb]], writes=[bRAWQ])

                    def summ(e, n=len(cols)):
                        ins = None
                        for ci in range(n):
                            ins = e.matmul(bank(6)[:, 0:bn], lhsT=ONES[:], rhs=SQ[:, ci, 0:bn],
                                           start=(ci == 0), stop=(ci == n - 1))
                        return ins
                    op("pe", summ, reads=[bSQ, bConst], writes=[bPS[6]])
                    rdst, brd = (RINVQ, bRINVQ) if grp == "q" else (TMPA, bTA)
                    op("act", lambda e: e.activation(out=rdst[:, 0:bn], in_=bank(6)[:, 0:bn], func=AF.Ln,
                                                     bias=EPST[:], scale=1.0 / nrm),
                       reads=[bPS[6], bConst], writes=[brd])
                    op("act", lambda e: e.activation(out=rdst[:, 0:bn], in_=rdst[:, 0:bn], func=AF.Exp, scale=-0.5),
                       reads=[brd], writes=[brd])
                    if grp == "kv":
                        pb = pbs[0]
                        op("dve", lambda e: e.scalar_tensor_tensor(
                            out=CKVT[:, bs:bs + bn], in0=bank(pb)[:, 0:bn], scalar=VEC[:, 2:3], in1=TMPA[:, 0:bn],
                            op0=ALU.mult, op1=ALU.mult), reads=[bPS[pb], bTA, bVec], writes=[bCKV[bi]])
                    yield
                pa = fm_banks.next()
                op("pe", lambda e: fm_mm(e, pa, lambda k: WIN[:, k, 320:416], 96),
                   reads=[bWIN, *bH1], writes=[bPS[pa]])
                if bs < CTX:
                    op("act", lambda e: e.activation(out=KRT[64:96, bs:bs + bn], in_=bank(pa)[64:96, 0:bn],
                                                     func=AF.Copy), reads=[bPS[pa]], writes=[bKR[bi]])
                    yield
                else:
                    pbk = fm_banks.next()
                    op("pe", lambda e: fm_mm(e, pbk, lambda k: WKB[:, k * 32:k * 32 + 96], 96),
                       reads=[bWKB, *bH1], writes=[bPS[pbk]])
                    t0 = bs - CTX
                    op("dve", lambda e: e.tensor_tensor(out=TMPA[64:96, 0:bn], in0=bank(pa)[64:96, 0:bn],
                                                        in1=COS[64:96, t0:t0 + bn], op=ALU.mult),
                       reads=[bPS[pa], bConst], writes=[bTA])
                    op("dve", lambda e: e.tensor_tensor(out=TMPB[64:96, 0:bn], in0=bank(pbk)[64:96, 0:bn],
                                                        in1=SIN[64:96, t0:t0 + bn], op=ALU.mult),
                       reads=[bPS[pbk], bConst], writes=[bTB])
                    op("dve", lambda e: e.tensor_tensor(out=KRT[64:96, bs:bs + bn], in0=TMPA[64:96, 0:bn],
                                                        in1=TMPB[64:96, 0:bn], op=ALU.add),
                       reads=[bTA, bTB], writes=[bKR[bi]])
                    yield
                for ci in range(2):
                    pb = fm_banks.next()
                    c0 = 928 + ci * 128
                    op("pe", lambda e: fm_mm(e, pb, lambda k: WIN[:, k, c0:c0 + 128], 128),
                       reads=[bWIN, *bH1], writes=[bPS[pb]])
                    pc0 = pcol(bs)
                    op("act", lambda e: e.activation(out=PT[:, ci, pc0:pc0 + bn], in_=bank(pb)[:, 0:bn], func=AF.Copy),
                       reads=[bPS[pb]], writes=[bPT])
                    yield
                for ci in range(2):
                    op("dve", lambda e: e.scalar_tensor_tensor(
                        out=CQT[:, ci, bs:bs + bn], in0=RAWQ[:, ci, 0:bn], scalar=VEC[:, ci:ci + 1], in1=RINVQ[:, 0:bn],
                        op0=ALU.mult, op1=ALU.mult), reads=[bRAWQ, bRINVQ, bVec], writes=[bCQ[bi]])
                yield

            cstate = {}

            def stageC(bi, jj):
                bs, bn = BLOCKS[bi]
                H1T, bH1 = H1Ts[bi % 2]
                j = bs // 128 + jj
                tb = tm_banks.next()
                VN32, VN, STMP, bVN = VNs.next()
                q, bq = lnring.next()
                st, mv = LST[:, q, :, :], LMV[:, q, :]
                cstate[j] = (VN, STMP, bVN)

                def tm(e):
                    ins = None
                    for half, c0 in ((0, 672), (1, 1184)):
                        for k in range(8):
                            ins = e.matmul(bank(tb)[:, half * 256:(half + 1) * 256],
                                           lhsT=H1T[:, k, jj * 128:(jj + 1) * 128], rhs=WIN[:, k, c0:c0 + 256],
                                           start=(k == 0), stop=(k == 7))
                    return ins
                op("pe", tm, reads=[bWIN, *bH1], writes=[bPS[tb]])
                op("dve", lambda e: e.bn_stats(out=st[:, 0, :], in_=bank(tb)[:, 0:256]), reads=[bPS[tb]], writes=[bq])
                op("act", lambda e: e.activation(out=FTOK[:, j, :], in_=bank(tb)[:, 256:512], func=AF.Copy),
                   reads=[bPS[tb]], writes=[bFT[j]])
                op("dve", lambda e: e.bn_aggr(out=mv[:, 0:2], in_=st[:, 0:1, :]), reads=[bq], writes=[bq])
                op("act", lambda e: e.activation(out=mv[:, 3:4], in_=mv[:, 1:2], func=AF.Ln, bias=EPST[:], scale=1.0),
                   reads=[bq, bConst], writes=[bq])
                op("act", lambda e: e.activation(out=mv[:, 2:3], in_=mv[:, 3:4], func=AF.Exp, scale=-0.5),
                   reads=[bq], writes=[bq])
                op("dve", lambda e: e.tensor_scalar(out=VN32, in0=bank(tb)[:, 0:256], scalar1=mv[:, 0:1],
                                                    scalar2=mv[:, 2:3], op0=ALU.subtract, op1=ALU.mult),
                   reads=[bq, bPS[tb]], writes=[bVN])
                op("dve", lambda e: e.tensor_tensor(out=VN32, in0=VN32, in1=SGB[:, 0, :], op=ALU.mult),
                   reads=[bVN, bSGB], writes=[bVN])
                op("dve", lambda e: e.tensor_tensor(out=VN, in0=VN32, in1=SGB[:, 1, :], op=ALU.add),
                   reads=[bVN, bSGB], writes=[bVN])

            def stageD(bi, jj):
                bs, bn = BLOCKS[bi]
                UT, bUT = UTs[bi % 2]
                j = bs // 128 + jj
                VN, STMP, bVN = cstate.pop(j)

                def sp(e):
                    ins = None
                    for g in range(4):
                        ins = e.matmul(bank(6)[:, g * 128:(g + 1) * 128], lhsT=VN[:, (g // 2) * 128:(g // 2) * 128 + 128],
                                       rhs=WST[:, g, :], start=True, stop=True)
                    return ins
                op("pe", sp, reads=[bVN, bWST], writes=[bPS[6]])
                for par in range(2):
                    r0 = par * 64
                    ps3 = bank(6)[r0:r0 + 64, :].rearrange("p (g c) -> p g c", g=4)
                    op("dve", lambda e: e.tensor_tensor(
                        out=STMP[r0:r0 + 64, :, :], in0=ps3[:, par::2, :], in1=BSB[r0:r0 + 64, par::2, :], op=ALU.add),
                       reads=[bPS[6], bSGB], writes=[bVN])
                    op("dve", lambda e: e.tensor_tensor(
                        out=R[r0:r0 + 64, 4:6, j * 128:(j + 1) * 128], in0=STMP[r0:r0 + 64, :, :],
                        in1=UT[r0:r0 + 64, :, jj * 128:(jj + 1) * 128], op=ALU.mult),
                       reads=[bVN, bUT], writes=[bMIX[bi]])

            for _ in stageA(0):
                pass
            for bi in range(len(BLOCKS)):
                gA = stageA(bi + 1) if bi + 1 < len(BLOCKS) else iter(())
                gB = stageB(bi)
                ntl = BLOCKS[bi][1] // 128
                stageC(bi, 0)
                next(gB, None)
                next(gB, None)
                for jj in range(ntl):
                    if jj + 1 < ntl:
                        stageC(bi, jj + 1)
                    next(gA, None)
                    next(gB, None)
                    next(gB, None)
                    stageD(bi, jj)
                for _ in gB:
                    pass
                for _ in gA:
                    pass
            if l == 0:
                tap("cqT", CQT[:, 0, :], bCQ)
                tap("ckvT", CKVT, bCKV)
            S.barrier()

            if stop == "p1":
                S.emit()
                return nc
            PA = aview(36992, [TP]); PBf = aview(41664, [TP])
            DTs = [(aview(46336, [TP]), Buf("dt0")), (aview(66368, [TP]), Buf("dt1"))]
            bPA, bPBf = Buf("pa"), Buf("pbf")
            DFR = [(aview(51008 + i * 1024, [512]), Buf("dfr%d" % i)) for i in range(8)]
            UW = aview(71024, [4, 512]); bUW = Buf("uw")
            MC = aview(63296, [2, 256]); MS = aview(64320, [2, 256]); WF = aview(65344, [2, 256])
            bMC, bWF = Buf("mc"), Buf("wf")
            dma("pool", WF, w_fou[l].rearrange("(k p) n -> p k n", p=128), writes=[bWF])
            for M_, BD_ in ((MC, BDC), (MS, BDS)):
                def mcm(e, BD_=BD_):
                    ins = None
                    for jx in range(2):
                        ins = e.matmul(bank(6)[:, jx * 256:(jx + 1) * 256], lhsT=BD_[:], rhs=WF[:, jx, :],
                                       start=True, stop=True)
                    return ins
                op("pe", mcm, reads=[bWF, bConst], writes=[bPS[6]])
                op("act", lambda e, M_=M_: e.activation(out=M_.rearrange("p a b -> p (a b)"), in_=bank(6), func=AF.Copy),
                   reads=[bPS[6]], writes=[bMC])
            for c in range(2):
                p_c = PT[:, c, :]
                DT, bDT = DTs[c]
                op("dve", lambda e, p_c=p_c: e.tensor_tensor(out=PA[:, 1:TP], in0=p_c[:, 0:TP - 1], in1=p_c[:, 1:TP],
                                                             op=ALU.add), reads=[bPT], writes=[bPA])
                op("dve", lambda e: e.tensor_tensor(out=PBf[:, 1:TP - 1], in0=PA[:, 0:TP - 2], in1=PA[:, 2:TP],
                                                    op=ALU.add), reads=[bPA], writes=[bPBf])
                if c == 0:
                    srcs = ((PA, bPA, 0, 2, 0), (PBf, bPBf, 64, 4, 1))
                else:
                    op("dve", lambda e: e.tensor_tensor(out=PA[:, 2:TP - 2], in0=PBf[:, 0:TP - 4], in1=PBf[:, 4:TP],
                                                        op=ALU.add), reads=[bPBf], writes=[bPA])
                    op("dve", lambda e: e.tensor_tensor(out=PBf[:, 4:TP - 4], in0=PA[:, 0:TP - 8], in1=PA[:, 8:TP],
                                                        op=ALU.add), reads=[bPA], writes=[bPBf])
                    srcs = ((PA, bPA, 0, 8, 2), (PBf, bPBf, 64, 16, 3))
                for (Sb, bSb, r0, wdw, g) in srcs:
                    hw = wdw // 2
                    for (sbeg, slen) in ((8, CTX), (8 + CTX + 8, SEQ)):
                        op("dve", lambda e, Sb=Sb, r0=r0, g=g, sbeg=sbeg, hw=hw: e.tensor_tensor(
                            out=Sb[r0:r0 + 64, sbeg:sbeg + hw], in0=Sb[r0:r0 + 64, sbeg:sbeg + hw],
                            in1=EDGE[r0:r0 + 64, g * 16:g * 16 + hw], op=ALU.mult), reads=[bSb, bConst], writes=[bSb])
                        if hw > 1:
                            e0 = sbeg + slen - (hw - 1)
                            op("dve", lambda e, Sb=Sb, r0=r0, g=g, e0=e0, hw=hw: e.tensor_tensor(
                                out=Sb[r0:r0 + 64, e0:e0 + hw - 1], in0=Sb[r0:r0 + 64, e0:e0 + hw - 1],
                                in1=EDGE[r0:r0 + 64, g * 16 + 8:g * 16 + 8 + hw - 1], op=ALU.mult),
                               reads=[bSb, bConst], writes=[bSb])
                    op("dve", lambda e, Sb=Sb, r0=r0, wdw=wdw, p_c=p_c: e.scalar_tensor_tensor(
                        out=DT[r0:r0 + 64, 8:TP - 8], in0=Sb[r0:r0 + 64, 8:TP - 8], scalar=1.0 / wdw,
                        in1=p_c[r0:r0 + 64, 8:TP - 8], op0=ALU.mult, op1=ALU.subtract),
                       reads=[bSb, bPT], writes=[bDT])
            dfr = Ring(DFR)

            def fourier_finish(bi, bs, bn):
                for q in range(4):
                    op("act", lambda e, q=q: e.activation(out=UW[:, q, 0:bn], in_=bank(q)[:, 0:bn], func=AF.Copy),
                       reads=[bPS[q]], writes=[bUW])
                for n in range(2):
                    def fin(e, n=n):
                        ins = None
                        i = 0
                        for (M_, q0) in ((MC, 0), (MS, 2)):
                            for jx in range(2):
                                ins = e.matmul(bank(4 + n)[:, 0:bn], lhsT=M_[:, jx, n * 128:(n + 1) * 128],
                                               rhs=UW[:, q0 + jx, 0:bn], start=(i == 0), stop=(i == 3))
                                i += 1
                        return ins
                    op("pe", fin, reads=[bUW, bMC], writes=[bPS[4 + n]])
                    op("act", lambda e, n=n: e.activation(out=R[:, 8 + n, bs:bs + bn], in_=bank(4 + n)[:, 0:bn],
                                                          func=AF.Copy), reads=[bPS[4 + n]], writes=[bMIX[bi]])
            if not last:
                for t in range(2):
                    def cx(e, t=t):
                        ins = None
                        for (tab, q0) in ((C256, 0), (S256, 2)):
                            for jx in range(2):
                                ins = e.matmul(bank(q0 + jx)[:, 0:256], lhsT=FTOK[:, t, jx * 128:(jx + 1) * 128],
                                               rhs=tab[:, t, :], start=(t == 0), stop=(t == 1))
                        return ins
                    op("pe", cx, reads=[bFT[t], bConst], writes=[bPS[0], bPS[1], bPS[2], bPS[3]])
                fourier_finish(0, 0, 256)
            for kb in range(4):
                for t in range(16):
                    (ct, cb), (sn, sb) = dfr.next(), dfr.next()
                    dma("pool", ct, k_cL[t * 128:(t + 1) * 128, kb * 512:(kb + 1) * 512], writes=[cb])
                    dma("pool", sn, k_sL[t * 128:(t + 1) * 128, kb * 512:(kb + 1) * 512], writes=[sb])

                    def lx(e, t=t, ct=ct, sn=sn):
                        ins = None
                        for (tab, q0) in ((ct, 0), (sn, 2)):
                            for jx in range(2):
                                ins = e.matmul(bank(q0 + jx)[:, :], lhsT=FTOK[:, 2 + t, jx * 128:(jx + 1) * 128],
                                               rhs=tab, start=(t == 0), stop=(t == 15))
                        return ins
                    op("pe", lx, reads=[bFT[2 + t], cb, sb], writes=[bPS[0], bPS[1], bPS[2], bPS[3]])
                fourier_finish(1 + kb, CTX + kb * 512, 512)
            for c in range(2):
                DT, bDT = DTs[c]
                for bi, (bs, bn) in enumerate(BLOCKS):
                    pc0 = pcol(bs)
                    op("pe", lambda e, c=c, pc0=pc0, bn=bn: e.matmul(bank(4 + c)[:, 0:bn], lhsT=WPB[:, c, :],
                                                                     rhs=DT[:, pc0:pc0 + bn], start=True, stop=True),
                       reads=[bDT, bWPB], writes=[bPS[4 + c]])
                    op("act", lambda e, c=c, bs=bs, bn=bn: e.activation(out=R[:, 6 + c, bs:bs + bn], in_=bank(4 + c)[:, 0:bn],
                                                                       func=AF.Copy, scale=VEC[:, 3 + c:4 + c]),
                       reads=[bPS[4 + c], bVec], writes=[bMIX[bi]])
            S.barrier()

            if stop == "p2a":
                S.emit()
                return nc
            VA = aview(18432, [NT, 128]); VBt = aview(23040, [NT, 128])
            bV = [Buf("va"), Buf("vb")]
            KTs = [(aview(27648, [NTOK]), Buf("kt0")), (aview(56064, [NTOK]), Buf("kt1"))]
            QTs = [aview(32256, [NTOK]), aview(36864, [NTOK])]; bQT = [Buf("qt0"), Buf("qt1")]
            PTR = [(aview(41472 + i * 2048, [2, 512]), Buf("ptr%d" % i)) for i in range(2)]
            PTR.append((aview(60672, [2, 512]), Buf("ptr2")))
            WUQ = aview(45568, [2, 768]); WUQP = aview(48640, [2, 320])
            WUK = aview(49920, [512]); WUV = aview(50944, [512]); bWA = Buf("wattn")
            RDEN = aview(51968, [512], F32); bRD = Buf("rden")
            QTMP = aview(54016, [512], F32); bQTMP = Buf("qtmp")
            WOUT = aview(56064, [10, D]); bWOUT = Buf("wout")
            dma("pool", WUQ, w_uq[l].rearrange("(k p) n -> p k n", p=128), writes=[bWA])
            op("dve", lambda e: e.memset(WUQP, 0.0), writes=[bWA])
            dma("pool", WUQP[:, :, 64:320], w_uq_rp[l].rearrange("(k p) n -> p k n", p=128), writes=[bWA])
            dma("pool", WUK, w_uk[l], writes=[bWA])
            dma("pool", WUV, w_uv[l], writes=[bWA])
            bWOUTb = Buf("woutb")
            dma("pool", WOUT[:, 4:10, :], w_out[l][512:1280, :].rearrange("(k p) n -> p k n", p=128), writes=[bWOUTb])
            op("dve", lambda e: e.memset(VA[:, :, 64:128], 1.0), writes=[bV[0]])
            op("dve", lambda e: e.memset(VBt[:, :, 0:64], 1.0), writes=[bV[1]])
            qk_banks = Ring([6, 7])
            sc_pairs = Ring([0, 1])
            o_banks = Ring([4, 5])
            ptr = Ring(PTR)
            qblocks = BLOCKS[1:] if last else BLOCKS

            def prep(h):
                par = h % 2
                Vt, bVt = (VA, bV[0]) if par == 0 else (VBt, bV[1])
                v0 = 0 if par == 0 else 64
                QT, bQ = QTs[par], bQT[par]
                KTt, bKT = KTs[par]
                for bi, (bs, bn) in enumerate(BLOCKS):
                    pb = qk_banks.next()
                    op("pe", lambda e, pb=pb, bs=bs, bn=bn: e.matmul(bank(pb)[0:64, 0:bn], lhsT=WUK[:, h * 64:(h + 1) * 64],
                                                                     rhs=CKVT[:, bs:bs + bn], start=True, stop=True),
                       reads=[bWA, bCKV[bi]], writes=[bPS[pb]])
                    op("dve", lambda e, pb=pb, bs=bs, bn=bn: e.tensor_copy(out=KTt[0:64, bs:bs + bn], in_=bank(pb)[0:64, 0:bn]),
                       reads=[bPS[pb]], writes=[bKT])
                    yield
                op("dve", lambda e: e.tensor_copy(out=KTt[64:96, :], in_=KRT[64:96, :]), reads=bKR, writes=[bKT])
                yield
                for g0 in range(0, NT, 8):
                    ng = min(8, NT - g0)
                    pb = qk_banks.next()

                    def vm(e, pb=pb, g0=g0, ng=ng):
                        ins = None
                        for jx in range(ng):
                            ins = e.matmul(bank(pb)[:, jx * 64:(jx + 1) * 64], lhsT=CKVT[:, (g0 + jx) * 128:(g0 + jx + 1) * 128],
                                           rhs=WUV[:, h * 64:(h + 1) * 64], start=True, stop=True)
                        return ins
                    op("pe", vm, reads=[bWA] + bCKV, writes=[bPS[pb]])
                    op("dve", lambda e, pb=pb, g0=g0, ng=ng: e.tensor_copy(
                        out=Vt[:, g0:g0 + ng, v0:v0 + 64], in_=bank(pb)[:, 0:ng * 64].rearrange("p (a b) -> p a b", b=64)),
                       reads=[bPS[pb]], writes=[bVt])
                    yield
                for bi, (bs, bn) in enumerate(qblocks):
                    bidx = BLOCKS.index((bs, bn))
                    pa = qk_banks.next()

                    def qa(e, pa=pa, bs=bs, bn=bn):
                        ins = None
                        for k in range(2):
                            ins = e.matmul(bank(pa)[0:96, 0:bn], lhsT=WUQ[:, k, h * 96:(h + 1) * 96],
                                           rhs=CQT[:, k, bs:bs + bn], start=(k == 0), stop=(k == 1))
                        return ins
                    op("pe", qa, reads=[bWA, bCQ[bidx]], writes=[bPS[pa]])
                    if bs < CTX:
                        op("dve", lambda e, pa=pa, bs=bs, bn=bn: e.tensor_copy(out=QT[0:96, bs:bs + bn], in_=bank(pa)[0:96, 0:bn]),
                           reads=[bPS[pa]], writes=[bQ])
                        yield
                    else:
                        op("dve", lambda e, pa=pa, bs=bs, bn=bn: e.tensor_copy(out=QT[0:64, bs:bs + bn], in_=bank(pa)[0:64, 0:bn]),
                           reads=[bPS[pa]], writes=[bQ])
                        pbk = qk_banks.next()

                        def qb(e, pbk=pbk, bs=bs, bn=bn):
                            ins = None
                            for k in range(2):
                                ins = e.matmul(bank(pbk)[0:96, 0:bn], lhsT=WUQP[:, k, h * 32:h * 32 + 96],
                                               rhs=CQT[:, k, bs:bs + bn], start=(k == 0), stop=(k == 1))
                            return ins
                        op("pe", qb, reads=[bWA, bCQ[bidx]], writes=[bPS[pbk]])
                        t0 = bs - CTX
                        op("dve", lambda e, pa=pa, t0=t0, bn=bn: e.tensor_tensor(
                            out=QTMP[64:96, 0:bn], in0=bank(pa)[64:96, 0:bn], in1=COS[64:96, t0:t0 + bn], op=ALU.mult),
                           reads=[bPS[pa], bConst], writes=[bQTMP])
                        op("dve", lambda e, pbk=pbk, t0=t0, bn=bn: e.tensor_tensor(
                            out=RDEN[64:96, 0:bn], in0=bank(pbk)[64:96, 0:bn], in1=SIN[64:96, t0:t0 + bn], op=ALU.mult),
                           reads=[bPS[pbk], bConst], writes=[bRD])
                        op("dve", lambda e, bs=bs, bn=bn: e.tensor_tensor(
                            out=QT[64:96, bs:bs + bn], in0=QTMP[64:96, 0:bn], in1=RDEN[64:96, 0:bn], op=ALU.add),
                           reads=[bQTMP, bRD], writes=[bQ])
                        yield

            def attend(h, pump=None):
                par = h % 2
                Vt, bVt = (VA, bV[0]) if par == 0 else (VBt, bV[1])
                QT, bQ = QTs[par], bQT[par]
                nr0, dr0 = (0, 64) if par == 0 else (64, 0)
                KTt, bKT = KTs[par]
                items = []
                for (bs, bn) in qblocks:
                    nk = 2 if bs < CTX else NT
                    for kp in range(nk // 2):
                        items.append((bs, bn, kp, nk // 2))
                sp_of = {}

                def issue_s(i):
                    bs, bn, kp, npair = items[i]
                    pw = sc_pairs.next()
                    sp_of[i] = pw

                    def smm(e):
                        ins = None
                        for u in range(2):
                            kc = 2 * kp + u
                            ins = e.matmul(PW[pw][:, u * 512:u * 512 + bn], lhsT=KTt[0:96, kc * 128:(kc + 1) * 128],
                                           rhs=QT[0:96, bs:bs + bn], start=True, stop=True)
                        return ins
                    op("pe", smm, reads=[bKT, bQ], writes=[bPS[2 * pw], bPS[2 * pw + 1]])
                issue_s(0)
                ob = None
                for i, (bs, bn, kp, npair) in enumerate(items):
                    pw = sp_of.pop(i)
                    pt, pbuf = ptr.next()
                    op("act", lambda e: e.activation(out=pt[:, :, 0:bn],
                                                     in_=PW[pw][:, :].rearrange("p (a b) -> p a b", a=2)[:, :, 0:bn],
                                                     func=AF.Exp, scale=96.0 ** -0.5),
                       reads=[bPS[2 * pw], bPS[2 * pw + 1]], writes=[pbuf])
                    if i + 1 < len(items):
                        issue_s(i + 1)
                    if kp == 0:
                        ob = o_banks.next()

                    def pv(e, ob=ob):
                        ins = None
                        for u in range(2):
                            kc = 2 * kp + u
                            ins = e.matmul(bank(ob)[:, 0:bn], lhsT=Vt[:, kc, :], rhs=pt[:, u, 0:bn],
                                           start=(kc == 0), stop=(kc == 2 * npair - 1))
                        return ins
                    op("pe", pv, reads=[bVt, pbuf], writes=[bPS[ob]])
                    if pump is not None and i % 2 == 1:
                        next(pump, None)
                    if kp == npair - 1:
                        bidx = BLOCKS.index((bs, bn))
                        op("dve", lambda e, ob=ob: e.reciprocal(out=RDEN[dr0:dr0 + 64, 0:bn],
                                                                in_=bank(ob)[dr0:dr0 + 64, 0:bn]),
                           reads=[bPS[ob]], writes=[bRD])
                        op("dve", lambda e, ob=ob: e.tensor_tensor(
                            out=R[nr0:nr0 + 64, h // 2, bs:bs + bn], in0=bank(ob)[nr0:nr0 + 64, 0:bn],
                            in1=RDEN[dr0:dr0 + 64, 0:bn], op=ALU.mult), reads=[bPS[ob], bRD], writes=[bMIX[bidx]])

            for _ in prep(0):
                pass
            for h in range(8):
                pump = prep(h + 1) if h + 1 < 8 else None
                attend(h, pump)
                if pump is not None:
                    for _ in pump:
                        pass
            if l == 0:
                tap("mixT", R[:, :, :].rearrange("p a b -> p (a b)"), bMIX)
            S.barrier()

            if stop == "p2b":
                S.emit()
                return nc
            BC = [aview(i * 4096, [D], F32) for i in range(4)]
            bBC = Buf("bc")
            TT = aview(16384, [D], F32); bTT = Buf("tt")
            ttr = Ring([(TT, bTT), (aview(20480, [D], F32), Buf("tt2"))])
            dma("pool", WOUT[:, 0:4, :], w_out[l][0:512, :].rearrange("(k p) n -> p k n", p=128), writes=[bWOUT])
            dma("sp", BC[2], ln1_g[l, :].partition_broadcast(128), writes=[bBC])
            dma("sp", BC[3], ln1_b[l, :].partition_broadcast(128), writes=[bBC])
            dma("sp", BC[0], gsc[l * 4 + 0, :].partition_broadcast(128), reads=[bG], writes=[bBC])
            dma("sp", BC[1], gsc[l * 4 + 1, :].partition_broadcast(128), reads=[bG], writes=[bBC])
            tiles = list(range(2, NT)) if last else list(range(NT))
            pwr = Ring([0, 1])
            lnq = []
            gM = iter(())
            if l + 1 < depth:
                WM3 = Ring([(aview(24576 + i * 8192, [8, 512]), Buf("wm%d" % i)) for i in range(3)])
                gM = mod_layer(l + 1, WM3, 4, (5, 6), aview(49152, [512], F32), aview(51200, [512], F32), Buf("brow"), Buf("grow"))
            for j in tiles:
                next(gM, None)
                w = 1 if j < 2 else 0
                bidx = 0 if j < 2 else 1 + (j - 2) // 4
                pwi = pwr.next()

                def om(e, pwi=pwi, j=j):
                    ins = None
                    korder = [4, 5, 6, 7, 8, 9, 0, 1, 2, 3]
                    for half in range(2):
                        for ki, k in enumerate(korder):
                            ins = e.matmul(PW[pwi][:, half * 512:(half + 1) * 512], lhsT=R[:, k, j * 128:(j + 1) * 128],
                                           rhs=WOUT[:, k, half * 512:(half + 1) * 512], start=(ki == 0), stop=(ki == 9))
                    return ins
                op("pe", om, reads=[bMIX[bidx], bWOUT, bWOUTb], writes=[bPS[2 * pwi], bPS[2 * pwi + 1]])
                tt, btt = ttr.next()
                op("dve", lambda e, pwi=pwi, w=w, tt=tt: e.tensor_tensor(out=tt, in0=PW[pwi][:, :], in1=BC[w], op=ALU.mult),
                   reads=[bPS[2 * pwi], bPS[2 * pwi + 1], bBC], writes=[btt])
                op("dve", lambda e, j=j, tt=tt: e.scalar_tensor_tensor(out=XS[:, j, :], in0=XS[:, j, :], scalar=ALPHA, in1=tt,
                                                                       op0=ALU.mult, op1=ALU.add),
                   reads=[bXS[j], btt], writes=[bXS[j]])
                lnq.append(("b", ln_a(j)))
                nxt = []
                for kind, c in lnq[:-1]:
                    if kind == "b":
                        nxt.append(("c", ln_b(c)))
                    else:
                        ln_c(c, BC[2], BC[3], bBC)
                lnq[:] = nxt + lnq[-1:]
            while lnq:
                nxt = []
                for kind, c in lnq:
                    if kind == "b":
                        nxt.append(("c", ln_b(c)))
                    else:
                        ln_c(c, BC[2], BC[3], bBC)
                lnq[:] = nxt
            for _ in gM:
                pass
            if l == 0:
                tap("x1", XS[:, 2, :], [bXS[2]])
            S.barrier()

            if stop == "p3":
                S.emit()
                return nc
            dma("sp", BC[2], ln2_g[l, :].partition_broadcast(128), writes=[bBC])
            dma("sp", BC[3], ln2_b[l, :].partition_broadcast(128), writes=[bBC])
            dma("sp", BC[0], gsc[l * 4 + 2, :].partition_broadcast(128), reads=[bG], writes=[bBC])
            dma("sp", BC[1], gsc[l * 4 + 3, :].partition_broadcast(128), reads=[bG], writes=[bBC])
            ttr = Ring([(TT, bTT)])
            FS = []
            for i in range(2):
                o = 20480 + i * 24576
                FS.append((aview(o, [8, 512]), aview(o + 8192, [8, 512]), aview(o + 16384, [4, D]), Buf("fs%d" % i),
                           Buf("fs2_%d" % i)))
            X16b = aview(69632, [D]); bX16b = Buf("x16b")
            SLT = [(aview(71680 + i * 2048, [512], F32), Buf("slt%d" % i)) for i in range(2)]
            HID = rview(8, 2, [4, 512]); bHID = Buf("hid")
            H2T = R[:, 0:8, :]
            bH2 = [(Buf("h2a_%d" % i), Buf("h2d_%d" % i)) for i in range(5)]
            fblocks = BLOCKS[1:] if last else BLOCKS
            def h2gen(bidx):
                bs, bn = BLOCKS[bidx]
                for jj in range(bn // 128):
                    j = bs // 128 + jj
                    make_hT(j, X16b, bX16b, H2T[:, :, j * 128:(j + 1) * 128], bH2[bidx], l, 4, 3, (7, 6))
                    yield
            fb_idx = [BLOCKS.index(b) for b in fblocks]
            for _ in h2gen(fb_idx[0]):
                pass
            fsr = Ring(FS)
            sltr = Ring(SLT)
            hbanks = Ring([0, 1, 2, 3])
            pw4 = Ring([2, 3])
            nsl = (D_FF + 511) // 512
            rem = D_FF - (nsl - 1) * 512
            pend_ln = []

            stage_ln = {"a": [], "b": [], "c": []}

            def out_tile(j):
                if l == depth - 1 and j >= 2:
                    dma("sp", out_d[(j - 2) * 128:(j - 1) * 128, :], XS[:, j, :], reads=[bXS[j]],
                        writes=[Buf("out%d" % j)], final=True)

            def ln_step():
                if stage_ln["c"]:
                    for c in stage_ln["c"]:
                        ln_c(c, BC[2], BC[3], bBC)
                        out_tile(c[0])
                    stage_ln["c"] = []
                if stage_ln["b"]:
                    stage_ln["c"] = [ln_b(c) for c in stage_ln["b"]]
                    stage_ln["b"] = []
                if stage_ln["a"]:
                    stage_ln["b"] = [ln_a(j) for j in stage_ln["a"]]
                    stage_ln["a"] = []

            def finish_tile():
                ln_step()
            for s in range(nsl):
                f0 = 0 if s == 0 else rem + (s - 1) * 512
                fw = rem if s == 0 else 512
                nfc = fw // 128
                W1S, W3S, W2S, bFS, bFS2 = fsr.next()
                dma("pool", W1S[:, :, 0:fw], w_f1[l][:, f0:f0 + fw].rearrange("(k p) n -> p k n", p=128), writes=[bFS])
                dma("pool", W3S[:, :, 0:fw], w_f3[l][:, f0:f0 + fw].rearrange("(k p) n -> p k n", p=128), writes=[bFS])
                dma("pool", W2S[:, 0:nfc, :], w_f2[l][f0:f0 + fw, :].rearrange("(c p) n -> p c n", p=128), writes=[bFS2])
                scaled = [False]

                def scale_w2():
                    for fc in range(nfc):
                        op("dve", lambda e: e.tensor_tensor(out=W2S[:, fc, :], in0=W2S[:, fc, :], in1=BC[0], op=ALU.mult),
                           reads=[bFS2, bBC], writes=[bFS2])
                    scaled[0] = True
                if last:
                    scale_w2()
                for (bs, bn) in fblocks:
                    bidx = BLOCKS.index((bs, bn))
                    n_prev = [0]
                    if pend_ln:
                        stage_ln["a"] = list(pend_ln)
                        pend_ln[:] = []
                        n_prev = [3]
                    if bs >= CTX and not scaled[0]:
                        scale_w2()
                    gH = iter(())
                    if s == 0 and fb_idx.index(bidx) + 1 < len(fb_idx):
                        gH = h2gen(fb_idx[fb_idx.index(bidx) + 1])
                    for fc in range(nfc):
                        next(gH, None)
                        b1, b3 = hbanks.next(), hbanks.next()

                        def hm(e, b1=b1, b3=b3, fc=fc, bs=bs, bn=bn, W1S=W1S, W3S=W3S):
                            ins = None
                            for (Wt, bb) in ((W1S, b1), (W3S, b3)):
                                for k in range(8):
                                    ins = e.matmul(bank(bb)[:, 0:bn], lhsT=Wt[:, k, fc * 128:(fc + 1) * 128],
                                                   rhs=H2T[:, k, bs:bs + bn], start=(k == 0), stop=(k == 7))
                            return ins
                        op("pe", hm, reads=[bFS, *bH2[bidx]], writes=[bPS[b1], bPS[b3]])
                        sl, bsl = sltr.next()
                        op("act", lambda e, b1=b1, sl=sl, bn=bn: e.activation(out=sl[:, 0:bn], in_=bank(b1)[:, 0:bn], func=AF.Silu),
                           reads=[bPS[b1]], writes=[bsl])
                        op("dve", lambda e, b3=b3, sl=sl, fc=fc, bn=bn: e.tensor_tensor(
                            out=HID[:, fc, 0:bn], in0=bank(b3)[:, 0:bn], in1=sl[:, 0:bn], op=ALU.mult),
                           reads=[bPS[b3], bsl], writes=[bHID])
                        if n_prev[0] > 0:
                            finish_tile()
                            n_prev[0] -= 1
                    for _ in gH:
                        pass
                    while n_prev[0] > 0:
                        finish_tile()
                        n_prev[0] -= 1
                    for jj in range(bn // 128):
                        j = bs // 128 + jj
                        w = 1 if j < 2 else 0

                        pwx = 2 if s == 0 else pw4.next()

                        def dm(e, jj=jj, nfc=nfc, W2S=W2S, pwx=pwx):
                            ins = None
                            for half in range(2):
                                for fc in range(nfc):
                                    ins = e.matmul(PW[pwx][:, half * 512:(half + 1) * 512], lhsT=HID[:, fc, jj * 128:(jj + 1) * 128],
                                                   rhs=W2S[:, fc, half * 512:(half + 1) * 512], start=(fc == 0), stop=(fc == nfc - 1))
                            return ins
                        op("pe", dm, reads=[bHID, bFS2], writes=[bPS[2 * pwx], bPS[2 * pwx + 1]])
                        if w == 1:
                            tt, btt = ttr.next()
                            op("dve", lambda e: e.tensor_tensor(out=tt, in0=PW[pwx][:, :], in1=BC[1], op=ALU.mult),
                               reads=[bPS[2 * pwx], bPS[2 * pwx + 1], bBC], writes=[btt])
                            if s == 0:
                                op("dve", lambda e: e.scalar_tensor_tensor(out=XS[:, j, :], in0=XS[:, j, :], scalar=ALPHA, in1=tt,
                                                                           op0=ALU.mult, op1=ALU.add),
                                   reads=[bXS[j], btt], writes=[bXS[j]])
                            else:
                                op("dve", lambda e: e.tensor_tensor(out=XS[:, j, :], in0=XS[:, j, :], in1=tt, op=ALU.add),
                                   reads=[bXS[j], btt], writes=[bXS[j]])
                        elif s == 0:
                            op("dve", lambda e: e.scalar_tensor_tensor(out=XS[:, j, :], in0=XS[:, j, :], scalar=ALPHA, in1=PW[pwx][:, :],
                                                                       op0=ALU.mult, op1=ALU.add),
                               reads=[bXS[j], bPS[2 * pwx], bPS[2 * pwx + 1]], writes=[bXS[j]])
                        else:
                            op("dve", lambda e: e.tensor_tensor(out=XS[:, j, :], in0=XS[:, j, :], in1=PW[pwx][:, :], op=ALU.add),
                               reads=[bXS[j], bPS[2 * pwx], bPS[2 * pwx + 1]], writes=[bXS[j]])
                        if s == nsl - 1:
                            pend_ln.append(j)
            if pend_ln:
                stage_ln["a"] = list(pend_ln)
                pend_ln[:] = []
            for _ in range(3):
                ln_step()
            S.barrier()
        S.emit()
    return nc


def _consts():
    f = np.float32
    c = {}
    c["k_ident"] = np.eye(128, dtype=f)
    i64 = np.arange(64)
    ang = 2.0 * np.pi * np.outer(i64, i64) / 64.0
    bdc = np.zeros((128, 128), f)
    bds = np.zeros((128, 128), f)
    for b in range(2):
        bdc[b * 64:(b + 1) * 64, b * 64:(b + 1) * 64] = np.cos(ang)
        bds[b * 64:(b + 1) * 64, b * 64:(b + 1) * 64] = np.sin(ang)
    c["k_bdc"], c["k_bds"] = bdc, bds
    t = np.arange(SEQ)
    row = (t // 64).astype(np.float64)
    col = (t % 64).astype(np.float64)
    inv = 10000.0 ** (-np.arange(8, dtype=np.float64) / 8.0)
    ar, ac = np.outer(inv, row), np.outer(inv, col)
    cosr, sinr, cosc, sinc = np.cos(ar), np.sin(ar), np.cos(ac), np.sin(ac)
    kc = np.zeros((128, SEQ), f)
    ks = np.zeros((128, SEQ), f)
    kc[64:72], kc[72:80], kc[80:88], kc[88:96] = cosr, cosr, cosc, cosc
    ks[64:72], ks[72:80], ks[80:88], ks[88:96] = -sinr, sinr, -sinc, sinc
    c["k_cos"], c["k_sin"] = kc, ks
    for L, nm in ((256, "256"), (SEQ, "L")):
        k = np.arange(L, dtype=np.int64)
        th = 2.0 * np.pi * ((np.outer(k, k) % L).astype(np.float64)) / L
        sc = 1.0 / math.sqrt(64.0 * L)
        c["k_c" + nm] = (np.cos(th) * sc).astype(f)
        c["k_s" + nm] = (-np.sin(th) * sc).astype(f)
    edge = np.ones((128, 64), f)
    for g, w in enumerate((2, 4, 8, 16)):
        hw = w // 2
        for i in range(hw):
            edge[:, g * 16 + i] = w / float(i + hw)
        for q in range(hw - 1):
            j = hw - 2 - q
            edge[:, g * 16 + 8 + q] = w / float(j + 1 + hw)
    c["k_edge"] = edge
    return c


_NC_CACHE = {}


def _prep_inputs(inp):
    f = np.float32
    A = lambda a: np.ascontiguousarray(np.asarray(a, dtype=f))
    sh = {}
    sh["w_mod"] = A(inp["w_mod"])
    sh["b_mod"] = A(inp["b_mod"])
    sh["b_modT"] = A(np.asarray(inp["b_mod"]).reshape(DEPTH, 48, 128).transpose(2, 0, 1).reshape(128, DEPTH * 48))
    w_in = np.asarray(inp["w_in"])
    sh["w_in"] = A(w_in)
    sh["w_in_krp"] = A(w_in[:, :, 384 + np.array(ROPE_PERM)])
    sh["q_normT"] = A(np.asarray(inp["q_norm"]).reshape(DEPTH, 2, 128).transpose(0, 2, 1))
    sh["kv_normT"] = A(np.asarray(inp["kv_norm"]).reshape(DEPTH, 1, 128).transpose(0, 2, 1))
    w_uq = np.asarray(inp["w_uq"])
    sh["w_uq"] = A(w_uq)
    cols = np.concatenate([h * 96 + 64 + np.array(ROPE_PERM) for h in range(8)])
    sh["w_uq_rp"] = A(w_uq[:, :, cols])
    sh["w_uk"] = A(inp["w_uk"])
    sh["w_uv"] = A(inp["w_uv"])
    sh["sgu_g"] = A(inp["sgu_ln_g"])
    sh["sgu_b"] = A(inp["sgu_ln_b"])
    sh["w_spT"] = A(np.asarray(inp["w_spatial"]).transpose(0, 1, 3, 2))
    sh["b_sp"] = A(np.asarray(inp["b_spatial"]).reshape(DEPTH, 512))
    sh["w_pool"] = A(inp["w_pool"])
    sh["pool_scT"] = A(np.asarray(inp["pool_scale"]).reshape(DEPTH, 2, 128).transpose(0, 2, 1))
    sh["w_fou"] = A(inp["w_fourier"])
    sh["w_out"] = A(inp["w_out"])
    sh["ln1_g"], sh["ln1_b"] = A(inp["ln1_g"]), A(inp["ln1_b"])
    sh["w_f1"], sh["w_f3"], sh["w_f2"] = A(inp["w_ffn1"]), A(inp["w_ffn3"]), A(inp["w_ffn2"])
    sh["ln2_g"], sh["ln2_b"] = A(inp["ln2_g"]), A(inp["ln2_b"])
    sh.update(_consts())
    x = np.asarray(inp["x"], dtype=f)
    ctx = np.asarray(inp["ctx"], dtype=f)
    c = np.asarray(inp["c"], dtype=f)
    cc = np.asarray(inp["c_ctx"], dtype=f)
    maps = []
    for b in range(8):
        m = dict(sh)
        m["x_in"] = np.ascontiguousarray(np.concatenate([ctx[b], x[b]], axis=0))
        cv = np.stack([c[b], cc], axis=0).reshape(2, 8, 128).transpose(2, 1, 0)
        m["cvec"] = np.ascontiguousarray(cv)
        maps.append(m)
    return maps


def kernel(**inputs):
    maps = _prep_inputs(inputs)
    if "nc" not in _NC_CACHE:
        _NC_CACHE["nc"] = build()
    res = run_bass_kernel_spmd(_NC_CACHE["nc"], maps, core_ids=list(range(8)))
    out = np.stack([np.asarray(r["out"], dtype=np.float32) for r in res.results], axis=0)
    return out
```

```python
import contextlib
import math
import numpy as np
import concourse.bass as bass
import concourse.mybir as mybir
from concourse.bass_utils import run_bass_kernel_spmd

F32 = mybir.dt.float32
BF16 = mybir.dt.bfloat16
AF = mybir.ActivationFunctionType
ALU = mybir.AluOpType

D = 1024
DEPTH = 4
SEQ = 2048
CTX = 256
NTOK = SEQ + CTX
NT = NTOK // 128
IN_DIM = 1440
D_FF = 2816
MIX = 1280
ALPHA = (2.0 * DEPTH) ** 0.25
EPS = 1e-6
TP = 8 + CTX + 8 + SEQ + 8
BLOCKS = [(0, 256), (256, 512), (768, 512), (1280, 512), (1792, 512)]
ROPE_PERM = list(range(8, 16)) + list(range(0, 8)) + list(range(24, 32)) + list(range(16, 24))


def pcol(tok):
    return tok + 8 if tok < CTX else tok + 16


class _Buf:
    __slots__ = ("name", "w", "readers", "dsem", "dcount", "excl")

    def __init__(self, name, excl=False):
        self.name = name
        self.excl = excl
        self.w = None
        self.readers = {}
        self.dsem = None
        self.dcount = 0


class Sched:
    ENGS = ("pe", "act", "dve", "pool", "sp")

    def __init__(self, nc, stack):
        self.nc = nc
        self.stack = stack
        self.sem, self.cnt, self.prog, self.seen = {}, {}, {}, {}
        for e in self.ENGS:
            self.sem[e] = stack.enter_context(nc.semaphore("s_" + e))
            self.cnt[e] = 0
            self.prog[e] = []
            self.seen[e] = {}
        self.final_bufs = []
        self.dma_bufs = {}
        self.nsem = 5
        self.engobj = {"pe": nc.tensor, "act": nc.scalar, "dve": nc.vector, "pool": nc.gpsimd, "sp": nc.sync}

    def sbuf(self, name, shape, dt):
        return self.stack.enter_context(self.nc.sbuf_tensor(name, list(shape), dt))

    def psum(self, name, shape, dt):
        return self.stack.enter_context(self.nc.psum_tensor(name, list(shape), dt))

    def _need(self, eng, dep, waits):
        if dep is None:
            return
        if dep[0] == "eng":
            key, val, sem = ("eng", dep[1]), dep[2], self.sem[dep[1]]
        else:
            key, val, sem = ("dma", id(dep[1])), dep[2], dep[1].dsem
        if self.seen[eng].get(key, 0) >= val:
            return
        self.seen[eng][key] = val
        waits.append((sem, val))

    def _deps(self, eng, reads, writes, is_dma=False):
        waits = []
        for b in reads:
            if b.w is not None:
                self._need(eng, b.w, waits)
            if b.excl:
                for k, r in b.readers.items():
                    if r[0] == "eng" and r[1] == eng:
                        continue
                    self._need(eng, r, waits)
        for b in writes:
            d = b.w
            if d is not None:
                if d[0] == "eng" and d[1] == eng and not is_dma and eng == "pe":
                    pass
                elif d[0] == "dma" and is_dma and not b.readers:
                    pass
                else:
                    self._need(eng, d, waits)
            for k, r in b.readers.items():
                if r[0] == "eng" and r[1] == eng and not is_dma and eng == "pe":
                    continue
                self._need(eng, r, waits)
        return waits

    def op(self, eng, fn, reads=(), writes=()):
        waits = self._deps(eng, reads, writes)
        self.cnt[eng] += 1
        me = ("eng", eng, self.cnt[eng])
        for b in reads:
            b.readers[eng] = me
        for b in writes:
            b.w = me
            b.readers = {}
        e = self.engobj[eng]
        for sem, val in waits:
            e.wait_ge(sem, val)
        fn(e).then_inc(self.sem[eng], 1)

    def dma(self, queue, out_ap, in_ap, reads=(), writes=(), final=False):
        waits = self._deps(queue, reads, writes, is_dma=True)
        b = writes[0]
        if b.dsem is None:
            b.dsem = self.stack.enter_context(self.nc.semaphore("d_" + b.name))
            self.nsem += 1
        b.dcount += 16
        me = ("dma", b, b.dcount)
        for r in reads:
            r.readers[("dma", id(b))] = me
        b.w = me
        b.readers = {}
        self.dma_bufs[id(b)] = b
        if final:
            self.final_bufs.append(b)

        e = self.engobj[queue]
        for sem, val in waits:
            e.wait_ge(sem, val)
        e.dma_start(out=out_ap, in_=in_ap).then_inc(b.dsem, 16)

    def barrier(self):
        snap = dict(self.cnt)
        for e in self.ENGS:
            waits = []
            for f in self.ENGS:
                if f != e and snap[f] > 0:
                    self._need(e, ("eng", f, snap[f]), waits)
            for b in self.dma_bufs.values():
                self._need(e, ("dma", b, b.dcount), waits)
            for sem, val in waits:
                self.engobj[e].wait_ge(sem, val)

    def emit(self):
        waits = []
        for b in self.final_bufs:
            self._need("sp", b.w, waits)
        for sem, val in waits:
            self.engobj["sp"].wait_ge(sem, val)


class Ring:
    def __init__(self, items):
        self.items = items
        self.i = 0

    def next(self):
        it = self.items[self.i % len(self.items)]
        self.i += 1
        return it


def build(depth=DEPTH, taps=(), stop=None):
    nc = bass.Bass("TRN2", target_bir_lowering=False)
    dr = {}
    _reg = {}

    def Buf(name, excl=False):
        if name not in _reg:
            _reg[name] = _Buf(name, excl)
        return _reg[name]

    def din(name, shape):
        dr[name] = nc.dram_tensor(name, list(shape), F32, kind="ExternalInput").ap()
        return dr[name]

    x_in = din("x_in", [NTOK, D])
    cvec = din("cvec", [128, 8, 2])
    w_mod = din("w_mod", [DEPTH, D, 6 * D])
    b_modT = din("b_modT", [128, DEPTH * 48])
    b_mod = din("b_mod", [DEPTH, 6 * D])
    w_in = din("w_in", [DEPTH, D, IN_DIM])
    w_in_krp = din("w_in_krp", [DEPTH, D, 32])
    q_normT = din("q_normT", [DEPTH, 128, 2])
    kv_normT = din("kv_normT", [DEPTH, 128, 1])
    w_uq = din("w_uq", [DEPTH, 256, 768])
    w_uq_rp = din("w_uq_rp", [DEPTH, 256, 256])
    w_uk = din("w_uk", [DEPTH, 128, 512])
    w_uv = din("w_uv", [DEPTH, 128, 512])
    sgu_g = din("sgu_g", [DEPTH, 256])
    sgu_b = din("sgu_b", [DEPTH, 256])
    w_spT = din("w_spT", [DEPTH, 4, 128, 128])
    b_sp = din("b_sp", [DEPTH, 512])
    w_pool = din("w_pool", [DEPTH, 4, 64, 64])
    pool_scT = din("pool_scT", [DEPTH, 128, 2])
    w_fou = din("w_fou", [DEPTH, 256, 256])
    w_out = din("w_out", [DEPTH, MIX, D])
    ln1_g = din("ln1_g", [DEPTH, D])
    ln1_b = din("ln1_b", [DEPTH, D])
    w_f1 = din("w_f1", [DEPTH, D, D_FF])
    w_f3 = din("w_f3", [DEPTH, D, D_FF])
    w_f2 = din("w_f2", [DEPTH, D_FF, D])
    ln2_g = din("ln2_g", [DEPTH, D])
    ln2_b = din("ln2_b", [DEPTH, D])
    k_ident = din("k_ident", [128, 128])
    k_bdc = din("k_bdc", [128, 128])
    k_bds = din("k_bds", [128, 128])
    k_cos = din("k_cos", [128, SEQ])
    k_sin = din("k_sin", [128, SEQ])
    k_c256 = din("k_c256", [256, 256])
    k_s256 = din("k_s256", [256, 256])
    k_cL = din("k_cL", [SEQ, SEQ])
    k_sL = din("k_sL", [SEQ, SEQ])
    k_edge = din("k_edge", [128, 64])
    out_d = nc.dram_tensor("out", [SEQ, D], F32, kind="ExternalOutput").ap()
    gsc = nc.dram_tensor("gscratch", [DEPTH * 4, D], F32, kind="Internal").ap()
    tap_d = {}
    for nm, shp in taps:
        tap_d[nm] = nc.dram_tensor("tap_" + nm, list(shp), F32, kind="ExternalOutput").ap()

    with contextlib.ExitStack() as st:
        S = Sched(nc, st)
        op, dma = S.op, S.dma

        XS = S.sbuf("XS", [128, NT, D], F32)
        bXS = [Buf("xs%d" % j) for j in range(NT)]
        R = S.sbuf("R", [128, 10, NTOK], BF16)
        ident = S.sbuf("ident", [128, 128], BF16)
        BDC = S.sbuf("BDC", [128, 128], BF16)
        BDS = S.sbuf("BDS", [128, 128], BF16)
        COS = S.sbuf("COS", [128, SEQ], BF16)
        SIN = S.sbuf("SIN", [128, SEQ], BF16)
        C256 = S.sbuf("C256", [128, 2, 256], BF16)
        S256 = S.sbuf("S256", [128, 2, 256], BF16)
        EDGE = S.sbuf("EDGE", [128, 64], F32)
        MODT = S.sbuf("MODT", [128, DEPTH * 48, 2], F32)
        OPSC = S.sbuf("OPSC", [128, DEPTH * 48, 2], F32)
        SILC = S.sbuf("SILC", [128, 8, 2], BF16)
        BMTp = S.sbuf("BMTp", [128, DEPTH * 48], F32)
        VEC = S.sbuf("VEC", [128, 8], F32)
        ST6 = S.sbuf("ST6", [128, 2, 6], F32)
        MV = S.sbuf("MV", [128, 2], F32)
        RS = S.sbuf("RS", [128, 1], F32)
        LST = S.sbuf("LST", [128, 4, 2, 6], F32)
        LMV = S.sbuf("LMV", [128, 4, 4], F32)
        lnring = Ring([(i, _Buf("lnst%d" % i)) for i in range(4)])
        EPST = S.sbuf("EPST", [128, 1], F32)
        ONES = S.sbuf("ONES", [128, 128], BF16)
        bConst = Buf("const")
        bMod = Buf("mod")
        bVec = Buf("vec")
        bSt = Buf("st")

        used = 0
        arena_elems = int(nc.sbuf_bytes_remaining) // 2 - 64
        AR = S.sbuf("ARENA", [128, arena_elems], BF16)
        ARB = arena_elems * 2

        def aview(off, shape, dt=BF16):
            n = int(np.prod(shape))
            nb = n * (2 if dt == BF16 else 4)
            assert off % 4 == 0 and off + nb <= ARB, ("arena overflow", off, nb, ARB)
            v = AR[:, off // 2: off // 2 + nb // 2]
            if dt == F32:
                v = v.bitcast(F32)
            if len(shape) == 2:
                v = v.rearrange("p (a b) -> p a b", a=shape[0])
            elif len(shape) == 3:
                v = v.rearrange("p (a b c) -> p a b c", a=shape[0], b=shape[1])
            return v

        def rview(c0, nchunk, shape, dt=BF16):
            n = int(np.prod(shape))
            nb = n * (2 if dt == BF16 else 4)
            assert nb <= nchunk * NTOK * 2
            v = R[:, c0:c0 + nchunk, :].rearrange("p a b -> p (a b)")[:, 0: nb // 2]
            if dt == F32:
                v = v.bitcast(F32)
            if len(shape) == 2:
                v = v.rearrange("p (a b) -> p a b", a=shape[0])
            elif len(shape) == 3:
                v = v.rearrange("p (a b c) -> p a b c", a=shape[0], b=shape[1])
            return v

        PW = [S.psum("PW%d" % i, [128, 1024], F32) for i in range(4)]
        PB7 = PW[3][:, 512:1024].bitcast(BF16)
        bPS = [Buf("ps%d" % i, excl=True) for i in range(8)]

        def bank(i):
            return PW[i // 2][:, (i % 2) * 512:(i % 2) * 512 + 512]

        dma("pool", ident[:], k_ident[:], writes=[bConst])
        dma("pool", BDC[:], k_bdc[:], writes=[bConst])
        dma("pool", BDS[:], k_bds[:], writes=[bConst])
        dma("pool", COS[:], k_cos[:], writes=[bConst])
        dma("pool", SIN[:], k_sin[:], writes=[bConst])
        dma("pool", C256[:], k_c256.rearrange("(k p) n -> p k n", p=128), writes=[bConst])
        dma("pool", S256[:], k_s256.rearrange("(k p) n -> p k n", p=128), writes=[bConst])
        dma("pool", EDGE[:], k_edge[:], writes=[bConst])
        op("dve", lambda e: e.memset(EPST[:], EPS), writes=[bConst])
        op("dve", lambda e: e.memset(ONES[:], 1.0), writes=[bConst])
        for j in range(NT):
            dma("sp", XS[:, j, :], x_in[j * 128:(j + 1) * 128, :], writes=[bXS[j]])

        if stop == "p0a":
            S.barrier(); S.emit()
            return nc
        CV = aview(0, [8, 2], F32)
        bCV = Buf("cv")
        dma("sp", CV, cvec[:], writes=[bCV])
        dma("sp", BMTp[:], b_modT[:], writes=[bCV])
        op("act", lambda e: e.activation(out=SILC[:], in_=CV, func=AF.Silu), reads=[bCV], writes=[bMod])
        bG = Buf("gsc")

        def mod_layer(l, ring, colbank, rowbanks, BROW, GROW, bBR, bGR):
            nring = len(ring.items)
            slots = {}

            def issue(pc):
                slots[pc] = ring.next()
                dma("pool", slots[pc][0], w_mod[l][:, pc * 512:(pc + 1) * 512].rearrange("(k p) n -> p k n", p=128),
                    writes=[slots[pc][1]])
            for pc in range(min(nring - 1, 12)):
                issue(pc)
            yield
            for pc in range(12):
                if pc + nring - 1 < 12:
                    issue(pc + nring - 1)
                wt, wb = slots.pop(pc)

                def mm(e):
                    ins = None
                    for mc in range(4):
                        col = (pc * 4 + mc) * 2
                        for k in range(8):
                            ins = e.matmul(bank(colbank)[:, col:col + 2], lhsT=wt[:, k, mc * 128:(mc + 1) * 128],
                                           rhs=SILC[:, k, :], start=(k == 0), stop=(k == 7))
                    return ins
                op("pe", mm, reads=[wb, bMod], writes=[bPS[colbank]])
                if pc in (4, 5, 10, 11):
                    gi, half = (0, pc - 4) if pc < 10 else (1, pc - 10)
                    rowbank = rowbanks[pc % 2]

                    def rowmm(e):
                        ins = None
                        for k in range(8):
                            ins = e.matmul(bank(rowbank)[0:2, 0:512], lhsT=SILC[:, k, :], rhs=wt[:, k, :],
                                           start=(k == 0), stop=(k == 7))
                        return ins
                    op("pe", rowmm, reads=[wb, bMod], writes=[bPS[rowbank]])
                    dma("sp", BROW[0:2, :], b_mod[l, pc * 512:(pc + 1) * 512].partition_broadcast(2), writes=[bBR])
                    op("dve", lambda e: e.tensor_tensor(out=GROW[0:2, :], in0=bank(rowbank)[0:2, 0:512], in1=BROW[0:2, :],
                                                        op=ALU.add), reads=[bPS[rowbank], bBR], writes=[bGR])
                    for w in range(2):
                        row = l * 4 + gi * 2 + w
                        dma("sp", gsc[row:row + 1, half * 512:(half + 1) * 512], GROW[w:w + 1, :], reads=[bGR], writes=[bG])
                yield
            for w in range(2):
                op("dve", lambda e: e.tensor_tensor(
                    out=MODT[:, l * 48:(l + 1) * 48, w],
                    in0=bank(colbank)[:, 0:96].rearrange("p (a b) -> p a b", b=2)[:, :, w],
                    in1=BMTp[:, l * 48:(l + 1) * 48], op=ALU.add), reads=[bPS[colbank], bCV], writes=[bMod])
            op("dve", lambda e: e.tensor_scalar_add(out=OPSC[:, l * 48:(l + 1) * 48, :], in0=MODT[:, l * 48:(l + 1) * 48, :],
                                                    scalar1=1.0), reads=[bMod], writes=[bMod])
            yield

        WM0 = Ring([(aview(4096 + i * 8192, [8, 512]), Buf("wm%d" % i)) for i in range(4)])
        for _ in mod_layer(0, WM0, 0, (1, 2), aview(36864, [512], F32), aview(38912, [512], F32), Buf("brow"), Buf("grow")):
            pass
        S.barrier()

        if stop == "p0":
            S.emit()
            return nc

        def modcol(l, part, c, w):
            i = l * 48 + part * 8 + c
            return MODT[:, i, w:w + 1]

        def opscol(l, part, c, w):
            i = l * 48 + part * 8 + c
            return OPSC[:, i, w:w + 1]

        def ln_a(j):
            q, bq = lnring.next()
            st, mv = LST[:, q, :, :], LMV[:, q, :]
            op("dve", lambda e: e.bn_stats(out=st[:, 0, :], in_=XS[:, j, 0:512]), reads=[bXS[j]], writes=[bq])
            op("dve", lambda e: e.bn_stats(out=st[:, 1, :], in_=XS[:, j, 512:1024]), reads=[bXS[j]], writes=[bq])
            op("dve", lambda e: e.bn_aggr(out=mv[:, 0:2], in_=st), reads=[bq], writes=[bq])
            op("act", lambda e: e.activation(out=mv[:, 2:3], in_=mv[:, 1:2], func=AF.Sqrt, bias=EPST[:], scale=1.0),
               reads=[bq, bConst], writes=[bq])
            return (j, mv, bq)

        def ln_b(c):
            j, mv, bq = c
            xt = XS[:, j, :]
            op("dve", lambda e: e.reciprocal(out=mv[:, 2:3], in_=mv[:, 2:3]), reads=[bq], writes=[bq])
            op("dve", lambda e: e.scalar_tensor_tensor(out=mv[:, 3:4], in0=mv[:, 0:1], scalar=-1.0, in1=mv[:, 2:3],
                                                       op0=ALU.mult, op1=ALU.mult), reads=[bq], writes=[bq])
            op("act", lambda e: e.activation(out=xt, in_=xt, func=AF.Identity, bias=mv[:, 3:4], scale=mv[:, 2:3]),
               reads=[bq, bXS[j]], writes=[bXS[j]])
            return c

        def ln_c(c, gbc, bbc, bGB):
            j = c[0]
            xt = XS[:, j, :]
            op("dve", lambda e: e.tensor_tensor(out=xt, in0=xt, in1=gbc, op=ALU.mult), reads=[bXS[j], bGB], writes=[bXS[j]])
            op("dve", lambda e: e.tensor_tensor(out=xt, in0=xt, in1=bbc, op=ALU.add), reads=[bXS[j], bGB], writes=[bXS[j]])

        def layer_norm_tile(j, gbc, bbc, bGB):
            ln_c(ln_b(ln_a(j)), gbc, bbc, bGB)

        def make_hT(j, X16, bX16, dst, bdsts, l, part_sc, part_sh, tb):
            w = 1 if j < 2 else 0
            pvs = [bank(tb[0]).bitcast(BF16), bank(tb[1]).bitcast(BF16)]
            op("act", lambda e: e.activation(out=X16, in_=XS[:, j, :], func=AF.Copy), reads=[bXS[j]], writes=[bX16])

            def tr(e):
                ins = None
                for c in range(8):
                    ins = e.transpose(pvs[c // 4][:, (c % 4) * 128:(c % 4 + 1) * 128], X16[:, c * 128:(c + 1) * 128], ident[:])
                return ins
            op("pe", tr, reads=[bX16, bConst], writes=[bPS[tb[0]], bPS[tb[1]]])
            for c in range(4):
                op("act", lambda e: e.activation(out=dst[:, c, :], in_=pvs[0][:, c * 128:(c + 1) * 128],
                                                 func=AF.Identity, bias=modcol(l, part_sh, c, w),
                                                 scale=opscol(l, part_sc, c, w)),
                   reads=[bPS[tb[0]], bMod], writes=[bdsts[0]])
                c2 = c + 4
                op("dve", lambda e: e.tensor_scalar(out=dst[:, c2, :], in0=pvs[1][:, c * 128:(c + 1) * 128],
                                                    scalar1=opscol(l, part_sc, c2, w),
                                                    scalar2=modcol(l, part_sh, c2, w),
                                                    op0=ALU.mult, op1=ALU.add),
                   reads=[bPS[tb[1]], bMod], writes=[bdsts[1]])

        def tap(name, src_ap, reads):
            if name in tap_d:
                dma("pool", tap_d[name], src_ap, reads=reads, writes=[Buf("tap_" + name)], final=True)

        CQT = aview(0, [2, NTOK]); bCQ = [Buf("cq%d" % i) for i in range(5)]
        CKVT = aview(9216, [NTOK]); bCKV = [Buf("ckv%d" % i) for i in range(5)]
        KRT = aview(13824, [NTOK]); bKR = [Buf("kr%d" % i) for i in range(5)]
        PT = aview(18432, [2, TP]); bPT = Buf("pT")
        FTOK = aview(27776, [NT, 256]); bFT = [Buf("ft%d" % j) for j in range(NT)]
        bMIX = [Buf("mix%d" % i) for i in range(5)]
        op("pool", lambda e: e.memset(PT, 0.0), writes=[bPT])
        S.barrier()

        for l in range(depth):
            last = (l == DEPTH - 1)
            WIN = aview(36992, [8, IN_DIM]); bWIN = Buf("win")
            WKB = aview(60032, [64 + 256]); bWKB = Buf("wkb")
            WST = aview(60672, [4, 128]); bWST = Buf("wst")
            WPB = aview(61696, [2, 128]); bWPB = Buf("wpb")
            SGB = aview(62208, [2, 256], F32); bSGB = Buf("sgb")
            BSB = aview(64256, [4, 128], F32)
            _r0 = rview(0, 4, [9216])
            _r0f = rview(0, 4, [4608], F32)
            _b16 = rview(6, 2, [4096])
            _b32 = rview(6, 2, [2048], F32)
            H1Ts = [(rview(8, 2, [8, 512]), (Buf("h1t0a"), Buf("h1t0d"))),
                    (_r0[:, 0:4096].rearrange("p (a b) -> p a b", a=8), (Buf("h1t1a"), Buf("h1t1d")))]
            X16s = Ring([(_b16[:, 0:1024], Buf("x16a")), (_r0[:, 4096:5120], Buf("x16b_"))])
            UTs = [(aview(66304, [2, 512]), Buf("ut0")), (_r0[:, 5120:6144].rearrange("p (a b) -> p a b", a=2), Buf("ut1"))]
            VNs = Ring([(aview(68352, [256], F32), aview(69376, [256]), aview(69888, [2, 128], F32), Buf("vn0")),
                        (_r0f[:, 3072:3328], _r0[:, 6656:6912],
                         _r0f[:, 3456:3712].rearrange("p (a b) -> p a b", a=2), Buf("vn1"))])
            TMPA = _b32[:, 512:1024]
            TMPB = _b32[:, 1024:1536]
            SQ = _b16[:, 3072:4096].rearrange("p (a b) -> p a b", a=2)
            bTA, bTB, bSQ = Buf("tmpa"), Buf("tmpb"), Buf("sq")
            RAWQ = aview(70912, [2, 512], F32); bRAWQ = Buf("rawq")
            RINVQ = aview(75008, [512], F32); bRINVQ = Buf("rinvq")
            RAWKV = _r0f[:, 3712:4224]; bRAWKV = Buf("rawkv")
            SQ32 = _b32[:, 1536:2048]
            tm_banks = Ring([3, 5])

            dma("pool", WIN, w_in[l].rearrange("(k p) n -> p k n", p=128), writes=[bWIN])
            op("dve", lambda e: e.memset(PT, 0.0), writes=[bPT])
            op("dve", lambda e: e.memset(WKB, 0.0), writes=[bWKB])
            dma("pool", WKB[:, 64:320].rearrange("p (k n) -> p k n", k=8),
                w_in_krp[l].rearrange("(k p) n -> p k n", p=128), writes=[bWKB])
            dma("pool", WST, w_spT[l].rearrange("g q p -> q g p"), writes=[bWST])
            op("dve", lambda e: e.memset(WPB, 0.0), writes=[bWPB])
            for g in range(4):
                r0 = (g % 2) * 64
                dma("pool", WPB[r0:r0 + 64, g // 2, r0:r0 + 64], w_pool[l, g], writes=[bWPB])
            dma("sp", SGB[:, 0, :], sgu_g[l, :].partition_broadcast(128), writes=[bSGB])
            dma("sp", SGB[:, 1, :], sgu_b[l, :].partition_broadcast(128), writes=[bSGB])
            dma("sp", BSB.rearrange("p a b -> p (a b)"), b_sp[l, :].partition_broadcast(128), writes=[bSGB])
            dma("sp", VEC[:, 0:2], q_normT[l], writes=[bVec])
            dma("sp", VEC[:, 2:3], kv_normT[l], writes=[bVec])
            dma("sp", VEC[:, 3:5], pool_scT[l], writes=[bVec])

            fm_banks = Ring([0, 1, 2])

            def stageA(bi):
                bs, bn = BLOCKS[bi]
                H1T, bH1 = H1Ts[bi % 2]
                for jj in range(bn // 128):
                    X16, bX16 = X16s.next()
                    make_hT(bs // 128 + jj, X16, bX16, H1T[:, :, jj * 128:(jj + 1) * 128], bH1, l, 1, 0, (7, 4))
                    yield

            def stageB(bi):
                bs, bn = BLOCKS[bi]
                H1T, bH1 = H1Ts[bi % 2]
                UT, bUT = UTs[bi % 2]

                def fm_mm(e, pb, lhs_fn, m):
                    ins = None
                    for k in range(8):
                        ins = e.matmul(bank(pb)[0:m, 0:bn], lhsT=lhs_fn(k), rhs=H1T[:, k, 0:bn],
                                       start=(k == 0), stop=(k == 7))
                    return ins
                for ci in range(2):
                    pb = fm_banks.next()
                    c0 = 416 + ci * 128
                    op("pe", lambda e: fm_mm(e, pb, lambda k: WIN[:, k, c0:c0 + 128], 128),
                       reads=[bWIN, *bH1], writes=[bPS[pb]])
                    op("act", lambda e: e.activation(out=UT[:, ci, 0:bn], in_=bank(pb)[:, 0:bn], func=AF.Copy),
                       reads=[bPS[pb]], writes=[bUT])
                    yield
                for grp, cols, nrm in (("q", (0, 128), 256.0), ("kv", (256,), 128.0)):
                    pbs = []
                    for ci, c0 in enumerate(cols):
                        pb = fm_banks.next()
                        pbs.append(pb)
                        op("pe", lambda e: fm_mm(e, pb, lambda k: WIN[:, k, c0:c0 + 128], 128),
                           reads=[bWIN, *bH1], writes=[bPS[pb]])
                        op("act", lambda e: e.activation(out=SQ[:, ci, 0:bn], in_=bank(pb)[:, 0:bn], func=AF.Square),
                           reads=[bPS[pb]], writes=[bSQ])
                        rawdst, braw = (RAWQ[:, ci, 0:bn], bRAWQ) if grp == "q" else (RAWKV[:, 0:bn], bRAWKV)
                        op("act", lambda e: e.activation(out=rawdst, in_=bank(pb)[:, 0:bn], func=AF.Copy),
                           reads=[bPS[pb]], writes=[braw])

                    def summ(e, n=len(cols)):
                        ins = None
                        for ci in range(n):
                            ins = e.matmul(bank(6)[:, 0:bn], lhsT=ONES[:], rhs=SQ[:, ci, 0:bn],
                                           start=(ci == 0), stop=(ci == n - 1))
                        return ins
                    op("pe", summ, reads=[bSQ, bConst], writes=[bPS[6]])
                    rdst, brd = (RINVQ, bRINVQ) if grp == "q" else (TMPA, bTA)
                    op("act", lambda e: e.activation(out=rdst[:, 0:bn], in_=bank(6)[:, 0:bn], func=AF.Ln,
                                                     bias=EPST[:], scale=1.0 / nrm),
                       reads=[bPS[6], bConst], writes=[brd])
                    op("act", lambda e: e.activation(out=rdst[:, 0:bn], in_=rdst[:, 0:bn], func=AF.Exp, scale=-0.5),
                       reads=[brd], writes=[brd])
                    yield
                pa = fm_banks.next()
                op("pe", lambda e: fm_mm(e, pa, lambda k: WIN[:, k, 320:416], 96),
                   reads=[bWIN, *bH1], writes=[bPS[pa]])
                if bs < CTX:
                    op("act", lambda e: e.activation(out=KRT[64:96, bs:bs + bn], in_=bank(pa)[64:96, 0:bn],
                                                     func=AF.Copy), reads=[bPS[pa]], writes=[bKR[bi]])
                    yield
                else:
                    pbk = fm_banks.next()
                    op("pe", lambda e: fm_mm(e, pbk, lambda k: WKB[:, k * 32:k * 32 + 96], 96),
                       reads=[bWKB, *bH1], writes=[bPS[pbk]])
                    t0 = bs - CTX
                    op("dve", lambda e: e.tensor_tensor(out=SQ32[64:96, 0:bn], in0=bank(pa)[64:96, 0:bn],
                                                        in1=COS[64:96, t0:t0 + bn], op=ALU.mult),
                       reads=[bPS[pa], bConst], writes=[bSQ])
                    op("dve", lambda e: e.tensor_tensor(out=TMPB[64:96, 0:bn], in0=bank(pbk)[64:96, 0:bn],
                                                        in1=SIN[64:96, t0:t0 + bn], op=ALU.mult),
                       reads=[bPS[pbk], bConst], writes=[bTB])
                    op("dve", lambda e: e.tensor_tensor(out=KRT[64:96, bs:bs + bn], in0=SQ32[64:96, 0:bn],
                                                        in1=TMPB[64:96, 0:bn], op=ALU.add),
                       reads=[bSQ, bTB], writes=[bKR[bi]])
                    yield
                for ci in range(2):
                    pb = fm_banks.next()
                    c0 = 928 + ci * 128
                    op("pe", lambda e: fm_mm(e, pb, lambda k: WIN[:, k, c0:c0 + 128], 128),
                       reads=[bWIN, *bH1], writes=[bPS[pb]])
                    pc0 = pcol(bs)
                    op("act", lambda e: e.activation(out=PT[:, ci, pc0:pc0 + bn], in_=bank(pb)[:, 0:bn], func=AF.Copy),
                       reads=[bPS[pb]], writes=[bPT])
                    yield
                for ci in range(2):
                    op("dve", lambda e: e.scalar_tensor_tensor(
                        out=CQT[:, ci, bs:bs + bn], in0=RAWQ[:, ci, 0:bn], scalar=VEC[:, ci:ci + 1], in1=RINVQ[:, 0:bn],
                        op0=ALU.mult, op1=ALU.mult), reads=[bRAWQ, bRINVQ, bVec], writes=[bCQ[bi]])
                op("dve", lambda e: e.scalar_tensor_tensor(
                    out=CKVT[:, bs:bs + bn], in0=RAWKV[:, 0:bn], scalar=VEC[:, 2:3], in1=TMPA[:, 0:bn],
                    op0=ALU.mult, op1=ALU.mult), reads=[bRAWKV, bTA, bVec], writes=[bCKV[bi]])
                yield

            cstate = {}

            def stageC(bi, jj):
                bs, bn = BLOCKS[bi]
                H1T, bH1 = H1Ts[bi % 2]
                j = bs // 128 + jj
                tb = tm_banks.next()
                VN32, VN, STMP, bVN = VNs.next()
                q, bq = lnring.next()
                st, mv = LST[:, q, :, :], LMV[:, q, :]
                cstate[j] = (VN, STMP, bVN)

                def tm(e):
                    ins = None
                    for half, c0 in ((0, 672), (1, 1184)):
                        for k in range(8):
                            ins = e.matmul(bank(tb)[:, half * 256:(half + 1) * 256],
                                           lhsT=H1T[:, k, jj * 128:(jj + 1) * 128], rhs=WIN[:, k, c0:c0 + 256],
                                           start=(k == 0), stop=(k == 7))
                    return ins
                op("pe", tm, reads=[bWIN, *bH1], writes=[bPS[tb]])
                op("dve", lambda e: e.bn_stats(out=st[:, 0, :], in_=bank(tb)[:, 0:256]), reads=[bPS[tb]], writes=[bq])
                op("act", lambda e: e.activation(out=FTOK[:, j, :], in_=bank(tb)[:, 256:512], func=AF.Copy),
                   reads=[bPS[tb]], writes=[bFT[j]])
                op("dve", lambda e: e.bn_aggr(out=mv[:, 0:2], in_=st[:, 0:1, :]), reads=[bq], writes=[bq])
                op("act", lambda e: e.activation(out=mv[:, 3:4], in_=mv[:, 1:2], func=AF.Ln, bias=EPST[:], scale=1.0),
                   reads=[bq, bConst], writes=[bq])
                op("act", lambda e: e.activation(out=mv[:, 2:3], in_=mv[:, 3:4], func=AF.Exp, scale=-0.5),
                   reads=[bq], writes=[bq])
                op("dve", lambda e: e.tensor_scalar(out=VN32, in0=bank(tb)[:, 0:256], scalar1=mv[:, 0:1],
                                                    scalar2=mv[:, 2:3], op0=ALU.subtract, op1=ALU.mult),
                   reads=[bq, bPS[tb]], writes=[bVN])
                op("dve", lambda e: e.tensor_tensor(out=VN32, in0=VN32, in1=SGB[:, 0, :], op=ALU.mult),
                   reads=[bVN, bSGB], writes=[bVN])
                op("dve", lambda e: e.tensor_tensor(out=VN, in0=VN32, in1=SGB[:, 1, :], op=ALU.add),
                   reads=[bVN, bSGB], writes=[bVN])

            def stageD(bi, jj):
                bs, bn = BLOCKS[bi]
                UT, bUT = UTs[bi % 2]
                j = bs // 128 + jj
                VN, STMP, bVN = cstate.pop(j)

                def sp(e):
                    ins = None
                    for g in range(4):
                        ins = e.matmul(bank(6)[:, g * 128:(g + 1) * 128], lhsT=VN[:, (g // 2) * 128:(g // 2) * 128 + 128],
                                       rhs=WST[:, g, :], start=True, stop=True)
                    return ins
                op("pe", sp, reads=[bVN, bWST], writes=[bPS[6]])
                for par in range(2):
                    r0 = par * 64
                    ps3 = bank(6)[r0:r0 + 64, :].rearrange("p (g c) -> p g c", g=4)
                    op("dve", lambda e: e.tensor_tensor(
                        out=STMP[r0:r0 + 64, :, :], in0=ps3[:, par::2, :], in1=BSB[r0:r0 + 64, par::2, :], op=ALU.add),
                       reads=[bPS[6], bSGB], writes=[bVN])
                    op("dve", lambda e: e.tensor_tensor(
                        out=R[r0:r0 + 64, 4:6, j * 128:(j + 1) * 128], in0=STMP[r0:r0 + 64, :, :],
                        in1=UT[r0:r0 + 64, :, jj * 128:(jj + 1) * 128], op=ALU.mult),
                       reads=[bVN, bUT], writes=[bMIX[bi]])

            for _ in stageA(0):
                pass
            for bi in range(len(BLOCKS)):
                gA = stageA(bi + 1) if bi + 1 < len(BLOCKS) else iter(())
                gB = stageB(bi)
                ntl = BLOCKS[bi][1] // 128
                stageC(bi, 0)
                next(gB, None)
                next(gB, None)
                for jj in range(ntl):
                    if jj + 1 < ntl:
                        stageC(bi, jj + 1)
                    next(gA, None)
                    next(gB, None)
                    next(gB, None)
                    stageD(bi, jj)
                for _ in gB:
                    pass
                for _ in gA:
                    pass
            if l == 0:
                tap("cqT", CQT[:, 0, :], bCQ)
                tap("ckvT", CKVT, bCKV)
            S.barrier()

            if stop == "p1":
                S.emit()
                return nc
            PA = aview(36992, [TP]); PBf = aview(41664, [TP])
            DTs = [(aview(46336, [TP]), Buf("dt0")), (aview(66368, [TP]), Buf("dt1"))]
            bPA, bPBf = Buf("pa"), Buf("pbf")
            DFR = [(aview(51008 + i * 1024, [512]), Buf("dfr%d" % i)) for i in range(8)]
            UW = aview(71024, [4, 512]); bUW = Buf("uw")
            MC = aview(63296, [2, 256]); MS = aview(64320, [2, 256]); WF = aview(65344, [2, 256])
            bMC, bWF = Buf("mc"), Buf("wf")
            dma("pool", WF, w_fou[l].rearrange("(k p) n -> p k n", p=128), writes=[bWF])
            for M_, BD_ in ((MC, BDC), (MS, BDS)):
                def mcm(e, BD_=BD_):
                    ins = None
                    for jx in range(2):
                        ins = e.matmul(bank(6)[:, jx * 256:(jx + 1) * 256], lhsT=BD_[:], rhs=WF[:, jx, :],
                                       start=True, stop=True)
                    return ins
                op("pe", mcm, reads=[bWF, bConst], writes=[bPS[6]])
                op("act", lambda e, M_=M_: e.activation(out=M_.rearrange("p a b -> p (a b)"), in_=bank(6), func=AF.Copy),
                   reads=[bPS[6]], writes=[bMC])
            for c in range(2):
                p_c = PT[:, c, :]
                DT, bDT = DTs[c]
                op("dve", lambda e, p_c=p_c: e.tensor_tensor(out=PA[:, 1:TP], in0=p_c[:, 0:TP - 1], in1=p_c[:, 1:TP],
                                                             op=ALU.add), reads=[bPT], writes=[bPA])
                op("dve", lambda e: e.tensor_tensor(out=PBf[:, 1:TP - 1], in0=PA[:, 0:TP - 2], in1=PA[:, 2:TP],
                                                    op=ALU.add), reads=[bPA], writes=[bPBf])
                if c == 0:
                    srcs = ((PA, bPA, 0, 2, 0), (PBf, bPBf, 64, 4, 1))
                else:
                    op("dve", lambda e: e.tensor_tensor(out=PA[:, 2:TP - 2], in0=PBf[:, 0:TP - 4], in1=PBf[:, 4:TP],
                                                        op=ALU.add), reads=[bPBf], writes=[bPA])
                    op("dve", lambda e: e.tensor_tensor(out=PBf[:, 4:TP - 4], in0=PA[:, 0:TP - 8], in1=PA[:, 8:TP],
                                                        op=ALU.add), reads=[bPA], writes=[bPBf])
                    srcs = ((PA, bPA, 0, 8, 2), (PBf, bPBf, 64, 16, 3))
                for (Sb, bSb, r0, wdw, g) in srcs:
                    hw = wdw // 2
                    for (sbeg, slen) in ((8, CTX), (8 + CTX + 8, SEQ)):
                        op("dve", lambda e, Sb=Sb, r0=r0, g=g, sbeg=sbeg, hw=hw: e.tensor_tensor(
                            out=Sb[r0:r0 + 64, sbeg:sbeg + hw], in0=Sb[r0:r0 + 64, sbeg:sbeg + hw],
                            in1=EDGE[r0:r0 + 64, g * 16:g * 16 + hw], op=ALU.mult), reads=[bSb, bConst], writes=[bSb])
                        if hw > 1:
                            e0 = sbeg + slen - (hw - 1)
                            op("dve", lambda e, Sb=Sb, r0=r0, g=g, e0=e0, hw=hw: e.tensor_tensor(
                                out=Sb[r0:r0 + 64, e0:e0 + hw - 1], in0=Sb[r0:r0 + 64, e0:e0 + hw - 1],
                                in1=EDGE[r0:r0 + 64, g * 16 + 8:g * 16 + 8 + hw - 1], op=ALU.mult),
                               reads=[bSb, bConst], writes=[bSb])
                    op("dve", lambda e, Sb=Sb, r0=r0, wdw=wdw, p_c=p_c: e.scalar_tensor_tensor(
                        out=DT[r0:r0 + 64, 8:TP - 8], in0=Sb[r0:r0 + 64, 8:TP - 8], scalar=1.0 / wdw,
                        in1=p_c[r0:r0 + 64, 8:TP - 8], op0=ALU.mult, op1=ALU.subtract),
                       reads=[bSb, bPT], writes=[bDT])
            dfr = Ring(DFR)

            def fourier_finish(bi, bs, bn):
                for q in range(4):
                    op("act", lambda e, q=q: e.activation(out=UW[:, q, 0:bn], in_=bank(q)[:, 0:bn], func=AF.Copy),
                       reads=[bPS[q]], writes=[bUW])
                for n in range(2):
                    def fin(e, n=n):
                        ins = None
                        i = 0
                        for (M_, q0) in ((MC, 0), (MS, 2)):
                            for jx in range(2):
                                ins = e.matmul(bank(4 + n)[:, 0:bn], lhsT=M_[:, jx, n * 128:(n + 1) * 128],
                                               rhs=UW[:, q0 + jx, 0:bn], start=(i == 0), stop=(i == 3))
                                i += 1
                        return ins
                    op("pe", fin, reads=[bUW, bMC], writes=[bPS[4 + n]])
                    op("act", lambda e, n=n: e.activation(out=R[:, 8 + n, bs:bs + bn], in_=bank(4 + n)[:, 0:bn],
                                                          func=AF.Copy), reads=[bPS[4 + n]], writes=[bMIX[bi]])
            if not last:
                for t in range(2):
                    def cx(e, t=t):
                        ins = None
                        for (tab, q0) in ((C256, 0), (S256, 2)):
                            for jx in range(2):
                                ins = e.matmul(bank(q0 + jx)[:, 0:256], lhsT=FTOK[:, t, jx * 128:(jx + 1) * 128],
                                               rhs=tab[:, t, :], start=(t == 0), stop=(t == 1))
                        return ins
                    op("pe", cx, reads=[bFT[t], bConst], writes=[bPS[0], bPS[1], bPS[2], bPS[3]])
                fourier_finish(0, 0, 256)
            for kb in range(4):
                for t in range(16):
                    (ct, cb), (sn, sb) = dfr.next(), dfr.next()
                    dma("pool", ct, k_cL[t * 128:(t + 1) * 128, kb * 512:(kb + 1) * 512], writes=[cb])
                    dma("pool", sn, k_sL[t * 128:(t + 1) * 128, kb * 512:(kb + 1) * 512], writes=[sb])

                    def lx(e, t=t, ct=ct, sn=sn):
                        ins = None
                        for (tab, q0) in ((ct, 0), (sn, 2)):
                            for jx in range(2):
                                ins = e.matmul(bank(q0 + jx)[:, :], lhsT=FTOK[:, 2 + t, jx * 128:(jx + 1) * 128],
                                               rhs=tab, start=(t == 0), stop=(t == 15))
                        return ins
                    op("pe", lx, reads=[bFT[2 + t], cb, sb], writes=[bPS[0], bPS[1], bPS[2], bPS[3]])
                fourier_finish(1 + kb, CTX + kb * 512, 512)
            for c in range(2):
                DT, bDT = DTs[c]
                for bi, (bs, bn) in enumerate(BLOCKS):
                    pc0 = pcol(bs)
                    op("pe", lambda e, c=c, pc0=pc0, bn=bn: e.matmul(bank(4 + c)[:, 0:bn], lhsT=WPB[:, c, :],
                                                                     rhs=DT[:, pc0:pc0 + bn], start=True, stop=True),
                       reads=[bDT, bWPB], writes=[bPS[4 + c]])
                    op("act", lambda e, c=c, bs=bs, bn=bn: e.activation(out=R[:, 6 + c, bs:bs + bn], in_=bank(4 + c)[:, 0:bn],
                                                                       func=AF.Copy, scale=VEC[:, 3 + c:4 + c]),
                       reads=[bPS[4 + c], bVec], writes=[bMIX[bi]])
            S.barrier()

            if stop == "p2a":
                S.emit()
                return nc
            VA = aview(18432, [NT, 128]); VBt = aview(23040, [NT, 128])
            bV = [Buf("va"), Buf("vb")]
            KTs = [(aview(27648, [NTOK]), Buf("kt0")), (aview(56064, [NTOK]), Buf("kt1"))]
            QTs = [aview(32256, [NTOK]), aview(36864, [NTOK])]; bQT = [Buf("qt0"), Buf("qt1")]
            PTR = [(aview(41472 + i * 2048, [2, 512]), Buf("ptr%d" % i)) for i in range(2)]
            PTR.append((aview(60672, [2, 512]), Buf("ptr2")))
            WUQ = aview(45568, [2, 768]); WUQP = aview(48640, [2, 320])
            WUK = aview(49920, [512]); WUV = aview(50944, [512]); bWA = Buf("wattn")
            RDEN = aview(51968, [512], F32); bRD = Buf("rden")
            QTMP = aview(54016, [512], F32); bQTMP = Buf("qtmp")
            WOUT = aview(56064, [10, D]); bWOUT = Buf("wout")
            dma("pool", WUQ, w_uq[l].rearrange("(k p) n -> p k n", p=128), writes=[bWA])
            op("dve", lambda e: e.memset(WUQP, 0.0), writes=[bWA])
            dma("pool", WUQP[:, :, 64:320], w_uq_rp[l].rearrange("(k p) n -> p k n", p=128), writes=[bWA])
            dma("pool", WUK, w_uk[l], writes=[bWA])
            dma("pool", WUV, w_uv[l], writes=[bWA])
            bWOUTb = Buf("woutb")
            dma("pool", WOUT[:, 4:10, :], w_out[l][512:1280, :].rearrange("(k p) n -> p k n", p=128), writes=[bWOUTb])
            op("dve", lambda e: e.memset(VA[:, :, 64:128], 1.0), writes=[bV[0]])
            op("dve", lambda e: e.memset(VBt[:, :, 0:64], 1.0), writes=[bV[1]])
            qk_banks = Ring([6, 7])
            sc_pairs = Ring([0, 1])
            o_banks = Ring([4, 5])
            ptr = Ring(PTR)
            qblocks = BLOCKS[1:] if last else BLOCKS

            def prep(h):
                par = h % 2
                Vt, bVt = (VA, bV[0]) if par == 0 else (VBt, bV[1])
                v0 = 0 if par == 0 else 64
                QT, bQ = QTs[par], bQT[par]
                KTt, bKT = KTs[par]
                for bi, (bs, bn) in enumerate(BLOCKS):
                    pb = qk_banks.next()
                    op("pe", lambda e, pb=pb, bs=bs, bn=bn: e.matmul(bank(pb)[0:64, 0:bn], lhsT=WUK[:, h * 64:(h + 1) * 64],
                                                                     rhs=CKVT[:, bs:bs + bn], start=True, stop=True),
                       reads=[bWA, bCKV[bi]], writes=[bPS[pb]])
                    op("dve", lambda e, pb=pb, bs=bs, bn=bn: e.tensor_copy(out=KTt[0:64, bs:bs + bn], in_=bank(pb)[0:64, 0:bn]),
                       reads=[bPS[pb]], writes=[bKT])
                    yield
                op("dve", lambda e: e.tensor_copy(out=KTt[64:96, :], in_=KRT[64:96, :]), reads=bKR, writes=[bKT])
                yield
                for g0 in range(0, NT, 8):
                    ng = min(8, NT - g0)
                    pb = qk_banks.next()

                    def vm(e, pb=pb, g0=g0, ng=ng):
                        ins = None
                        for jx in range(ng):
                            ins = e.matmul(bank(pb)[:, jx * 64:(jx + 1) * 64], lhsT=CKVT[:, (g0 + jx) * 128:(g0 + jx + 1) * 128],
                                           rhs=WUV[:, h * 64:(h + 1) * 64], start=True, stop=True)
                        return ins
                    op("pe", vm, reads=[bWA] + bCKV, writes=[bPS[pb]])
                    op("dve", lambda e, pb=pb, g0=g0, ng=ng: e.tensor_copy(
                        out=Vt[:, g0:g0 + ng, v0:v0 + 64], in_=bank(pb)[:, 0:ng * 64].rearrange("p (a b) -> p a b", b=64)),
                       reads=[bPS[pb]], writes=[bVt])
                    yield
                for bi, (bs, bn) in enumerate(qblocks):
                    bidx = BLOCKS.index((bs, bn))
                    pa = qk_banks.next()

                    def qa(e, pa=pa, bs=bs, bn=bn):
                        ins = None
                        for k in range(2):
                            ins = e.matmul(bank(pa)[0:96, 0:bn], lhsT=WUQ[:, k, h * 96:(h + 1) * 96],
                                           rhs=CQT[:, k, bs:bs + bn], start=(k == 0), stop=(k == 1))
                        return ins
                    op("pe", qa, reads=[bWA, bCQ[bidx]], writes=[bPS[pa]])
                    if bs < CTX:
                        op("dve", lambda e, pa=pa, bs=bs, bn=bn: e.tensor_copy(out=QT[0:96, bs:bs + bn], in_=bank(pa)[0:96, 0:bn]),
                           reads=[bPS[pa]], writes=[bQ])
                        yield
                    else:
                        op("dve", lambda e, pa=pa, bs=bs, bn=bn: e.tensor_copy(out=QT[0:64, bs:bs + bn], in_=bank(pa)[0:64, 0:bn]),
                           reads=[bPS[pa]], writes=[bQ])
                        pbk = qk_banks.next()

                        def qb(e, pbk=pbk, bs=bs, bn=bn):
                            ins = None
                            for k in range(2):
                                ins = e.matmul(bank(pbk)[0:96, 0:bn], lhsT=WUQP[:, k, h * 32:h * 32 + 96],
                                               rhs=CQT[:, k, bs:bs + bn], start=(k == 0), stop=(k == 1))
                            return ins
                        op("pe", qb, reads=[bWA, bCQ[bidx]], writes=[bPS[pbk]])
                        t0 = bs - CTX
                        op("dve", lambda e, pa=pa, t0=t0, bn=bn: e.tensor_tensor(
                            out=QTMP[64:96, 0:bn], in0=bank(pa)[64:96, 0:bn], in1=COS[64:96, t0:t0 + bn], op=ALU.mult),
                           reads=[bPS[pa], bConst], writes=[bQTMP])
                        op("dve", lambda e, pbk=pbk, t0=t0, bn=bn: e.tensor_tensor(
                            out=RDEN[64:96, 0:bn], in0=bank(pbk)[64:96, 0:bn], in1=SIN[64:96, t0:t0 + bn], op=ALU.mult),
                           reads=[bPS[pbk], bConst], writes=[bRD])
                        op("dve", lambda e, bs=bs, bn=bn: e.tensor_tensor(
                            out=QT[64:96, bs:bs + bn], in0=QTMP[64:96, 0:bn], in1=RDEN[64:96, 0:bn], op=ALU.add),
                           reads=[bQTMP, bRD], writes=[bQ])
                        yield

            def attend(h, pump=None):
                par = h % 2
                Vt, bVt = (VA, bV[0]) if par == 0 else (VBt, bV[1])
                QT, bQ = QTs[par], bQT[par]
                nr0, dr0 = (0, 64) if par == 0 else (64, 0)
                KTt, bKT = KTs[par]
                items = []
                for (bs, bn) in qblocks:
                    nk = 2 if bs < CTX else NT
                    for kp in range(nk // 2):
                        items.append((bs, bn, kp, nk // 2))
                sp_of = {}

                def issue_s(i):
                    bs, bn, kp, npair = items[i]
                    pw = sc_pairs.next()
                    sp_of[i] = pw

                    def smm(e):
                        ins = None
                        for u in range(2):
                            kc = 2 * kp + u
                            ins = e.matmul(PW[pw][:, u * 512:u * 512 + bn], lhsT=KTt[0:96, kc * 128:(kc + 1) * 128],
                                           rhs=QT[0:96, bs:bs + bn], start=True, stop=True)
                        return ins
                    op("pe", smm, reads=[bKT, bQ], writes=[bPS[2 * pw], bPS[2 * pw + 1]])
                issue_s(0)
                ob = None
                for i, (bs, bn, kp, npair) in enumerate(items):
                    pw = sp_of.pop(i)
                    pt, pbuf = ptr.next()
                    op("act", lambda e: e.activation(out=pt[:, :, 0:bn],
                                                     in_=PW[pw][:, :].rearrange("p (a b) -> p a b", a=2)[:, :, 0:bn],
                                                     func=AF.Exp, scale=96.0 ** -0.5),
                       reads=[bPS[2 * pw], bPS[2 * pw + 1]], writes=[pbuf])
                    if i + 1 < len(items):
                        issue_s(i + 1)
                    if kp == 0:
                        ob = o_banks.next()

                    def pv(e, ob=ob):
                        ins = None
                        for u in range(2):
                            kc = 2 * kp + u
                            ins = e.matmul(bank(ob)[:, 0:bn], lhsT=Vt[:, kc, :], rhs=pt[:, u, 0:bn],
                                           start=(kc == 0), stop=(kc == 2 * npair - 1))
                        return ins
                    op("pe", pv, reads=[bVt, pbuf], writes=[bPS[ob]])
                    if pump is not None and i % 2 == 1:
                        next(pump, None)
                    if kp == npair - 1:
                        bidx = BLOCKS.index((bs, bn))
                        op("dve", lambda e, ob=ob: e.reciprocal(out=RDEN[dr0:dr0 + 64, 0:bn],
                                                                in_=bank(ob)[dr0:dr0 + 64, 0:bn]),
                           reads=[bPS[ob]], writes=[bRD])
                        op("dve", lambda e, ob=ob: e.tensor_tensor(
                            out=R[nr0:nr0 + 64, h // 2, bs:bs + bn], in0=bank(ob)[nr0:nr0 + 64, 0:bn],
                            in1=RDEN[dr0:dr0 + 64, 0:bn], op=ALU.mult), reads=[bPS[ob], bRD], writes=[bMIX[bidx]])

            for _ in prep(0):
                pass
            for h in range(8):
                pump = prep(h + 1) if h + 1 < 8 else None
                attend(h, pump)
                if pump is not None:
                    for _ in pump:
                        pass
            if l == 0:
                tap("mixT", R[:, :, :].rearrange("p a b -> p (a b)"), bMIX)
            S.barrier()

            if stop == "p2b":
                S.emit()
                return nc
            BC = [aview(i * 4096, [D], F32) for i in range(4)]
            bBC = Buf("bc")
            TT = aview(16384, [D], F32); bTT = Buf("tt")
            ttr = Ring([(TT, bTT), (aview(20480, [D], F32), Buf("tt2"))])
            dma("pool", WOUT[:, 0:4, :], w_out[l][0:512, :].rearrange("(k p) n -> p k n", p=128), writes=[bWOUT])
            dma("sp", BC[2], ln1_g[l, :].partition_broadcast(128), writes=[bBC])
            dma("sp", BC[3], ln1_b[l, :].partition_broadcast(128), writes=[bBC])
            dma("sp", BC[0], gsc[l * 4 + 0, :].partition_broadcast(128), reads=[bG], writes=[bBC])
            dma("sp", BC[1], gsc[l * 4 + 1, :].partition_broadcast(128), reads=[bG], writes=[bBC])
            tiles = list(range(2, NT)) if last else list(range(NT))
            pwr = Ring([0, 1])
            lnq = []
            gM = iter(())
            if l + 1 < depth:
                WM3 = Ring([(aview(24576 + i * 8192, [8, 512]), Buf("wm%d" % i)) for i in range(3)])
                gM = mod_layer(l + 1, WM3, 4, (5, 6), aview(49152, [512], F32), aview(51200, [512], F32), Buf("brow"), Buf("grow"))
            for j in tiles:
                next(gM, None)
                w = 1 if j < 2 else 0
                bidx = 0 if j < 2 else 1 + (j - 2) // 4
                pwi = pwr.next()

                def om(e, pwi=pwi, j=j):
                    ins = None
                    korder = [4, 5, 6, 7, 8, 9, 0, 1, 2, 3]
                    for half in range(2):
                        for ki, k in enumerate(korder):
                            ins = e.matmul(PW[pwi][:, half * 512:(half + 1) * 512], lhsT=R[:, k, j * 128:(j + 1) * 128],
                                           rhs=WOUT[:, k, half * 512:(half + 1) * 512], start=(ki == 0), stop=(ki == 9))
                    return ins
                op("pe", om, reads=[bMIX[bidx], bWOUT, bWOUTb], writes=[bPS[2 * pwi], bPS[2 * pwi + 1]])
                tt, btt = ttr.next()
                op("dve", lambda e, pwi=pwi, w=w, tt=tt: e.tensor_tensor(out=tt, in0=PW[pwi][:, :], in1=BC[w], op=ALU.mult),
                   reads=[bPS[2 * pwi], bPS[2 * pwi + 1], bBC], writes=[btt])
                op("dve", lambda e, j=j, tt=tt: e.scalar_tensor_tensor(out=XS[:, j, :], in0=XS[:, j, :], scalar=ALPHA, in1=tt,
                                                                       op0=ALU.mult, op1=ALU.add),
                   reads=[bXS[j], btt], writes=[bXS[j]])
                lnq.append(("b", ln_a(j)))
                nxt = []
                for kind, c in lnq[:-1]:
                    if kind == "b":
                        nxt.append(("c", ln_b(c)))
                    else:
                        ln_c(c, BC[2], BC[3], bBC)
                lnq[:] = nxt + lnq[-1:]
            while lnq:
                nxt = []
                for kind, c in lnq:
                    if kind == "b":
                        nxt.append(("c", ln_b(c)))
                    else:
                        ln_c(c, BC[2], BC[3], bBC)
                lnq[:] = nxt
            for _ in gM:
                pass
            if l == 0:
                tap("x1", XS[:, 2, :], [bXS[2]])
            S.barrier()

            if stop == "p3":
                S.emit()
                return nc
            dma("sp", BC[2], ln2_g[l, :].partition_broadcast(128), writes=[bBC])
            dma("sp", BC[3], ln2_b[l, :].partition_broadcast(128), writes=[bBC])
            dma("sp", BC[0], gsc[l * 4 + 2, :].partition_broadcast(128), reads=[bG], writes=[bBC])
            dma("sp", BC[1], gsc[l * 4 + 3, :].partition_broadcast(128), reads=[bG], writes=[bBC])
            ttr = Ring([(TT, bTT)])
            FS = []
            for i in range(2):
                o = 20480 + i * 24576
                FS.append((aview(o, [8, 512]), aview(o + 8192, [8, 512]), aview(o + 16384, [4, D]), Buf("fs%d" % i),
                           Buf("fs2_%d" % i)))
            X16b = aview(69632, [D]); bX16b = Buf("x16b")
            SLT = [(aview(71680 + i * 2048, [512], F32), Buf("slt%d" % i)) for i in range(2)]
            HID = rview(8, 2, [4, 512]); bHID = Buf("hid")
            H2T = R[:, 0:8, :]
            bH2 = [(Buf("h2a_%d" % i), Buf("h2d_%d" % i)) for i in range(5)]
            fblocks = BLOCKS[1:] if last else BLOCKS
            def h2gen(bidx):
                bs, bn = BLOCKS[bidx]
                for jj in range(bn // 128):
                    j = bs // 128 + jj
                    make_hT(j, X16b, bX16b, H2T[:, :, j * 128:(j + 1) * 128], bH2[bidx], l, 4, 3, (7, 6))
                    yield
            fb_idx = [BLOCKS.index(b) for b in fblocks]
            for _ in h2gen(fb_idx[0]):
                pass
            fsr = Ring(FS)
            sltr = Ring(SLT)
            hbanks = Ring([0, 1, 2, 3])
            pw4 = Ring([2, 3])
            nsl = (D_FF + 511) // 512
            rem = D_FF - (nsl - 1) * 512
            pend_ln = []

            stage_ln = {"a": [], "b": [], "c": []}

            def out_tile(j):
                if l == depth - 1 and j >= 2:
                    dma("sp", out_d[(j - 2) * 128:(j - 1) * 128, :], XS[:, j, :], reads=[bXS[j]],
                        writes=[Buf("out%d" % j)], final=True)

            def ln_step():
                if stage_ln["c"]:
                    for c in stage_ln["c"]:
                        ln_c(c, BC[2], BC[3], bBC)
                        out_tile(c[0])
                    stage_ln["c"] = []
                if stage_ln["b"]:
                    stage_ln["c"] = [ln_b(c) for c in stage_ln["b"]]
                    stage_ln["b"] = []
                if stage_ln["a"]:
                    stage_ln["b"] = [ln_a(j) for j in stage_ln["a"]]
                    stage_ln["a"] = []

            def finish_tile():
                ln_step()
            for s in range(nsl):
                f0 = 0 if s == 0 else rem + (s - 1) * 512
                fw = rem if s == 0 else 512
                nfc = fw // 128
                W1S, W3S, W2S, bFS, bFS2 = fsr.next()
                dma("pool", W1S[:, :, 0:fw], w_f1[l][:, f0:f0 + fw].rearrange("(k p) n -> p k n", p=128), writes=[bFS])
                dma("pool", W3S[:, :, 0:fw], w_f3[l][:, f0:f0 + fw].rearrange("(k p) n -> p k n", p=128), writes=[bFS])
                dma("pool", W2S[:, 0:nfc, :], w_f2[l][f0:f0 + fw, :].rearrange("(c p) n -> p c n", p=128), writes=[bFS2])
                scaled = [False]

                def scale_w2():
                    for fc in range(nfc):
                        op("dve", lambda e: e.tensor_tensor(out=W2S[:, fc, :], in0=W2S[:, fc, :], in1=BC[0], op=ALU.mult),
                           reads=[bFS2, bBC], writes=[bFS2])
                    scaled[0] = True
                if last:
                    scale_w2()
                for (bs, bn) in fblocks:
                    bidx = BLOCKS.index((bs, bn))
                    n_prev = [0]
                    if pend_ln:
                        stage_ln["a"] = list(pend_ln)
                        pend_ln[:] = []
                        n_prev = [3]
                    if bs >= CTX and not scaled[0]:
                        scale_w2()
                    gH = iter(())
                    if s == 0 and fb_idx.index(bidx) + 1 < len(fb_idx):
                        gH = h2gen(fb_idx[fb_idx.index(bidx) + 1])
                    for fc in range(nfc):
                        next(gH, None)
                        b1, b3 = hbanks.next(), hbanks.next()

                        def hm(e, b1=b1, b3=b3, fc=fc, bs=bs, bn=bn, W1S=W1S, W3S=W3S):
                            ins = None
                            for (Wt, bb) in ((W1S, b1), (W3S, b3)):
                                for k in range(8):
                                    ins = e.matmul(bank(bb)[:, 0:bn], lhsT=Wt[:, k, fc * 128:(fc + 1) * 128],
                                                   rhs=H2T[:, k, bs:bs + bn], start=(k == 0), stop=(k == 7))
                            return ins
                        op("pe", hm, reads=[bFS, *bH2[bidx]], writes=[bPS[b1], bPS[b3]])
                        sl, bsl = sltr.next()
                        op("act", lambda e, b1=b1, sl=sl, bn=bn: e.activation(out=sl[:, 0:bn], in_=bank(b1)[:, 0:bn], func=AF.Silu),
                           reads=[bPS[b1]], writes=[bsl])
                        op("dve", lambda e, b3=b3, sl=sl, fc=fc, bn=bn: e.tensor_tensor(
                            out=HID[:, fc, 0:bn], in0=bank(b3)[:, 0:bn], in1=sl[:, 0:bn], op=ALU.mult),
                           reads=[bPS[b3], bsl], writes=[bHID])
                        if n_prev[0] > 0:
                            finish_tile()
                            n_prev[0] -= 1
                    for _ in gH:
                        pass
                    while n_prev[0] > 0:
                        finish_tile()
                        n_prev[0] -= 1
                    for jj in range(bn // 128):
                        j = bs // 128 + jj
                        w = 1 if j < 2 else 0

                        pwx = 2 if s == 0 else pw4.next()

                        def dm(e, jj=jj, nfc=nfc, W2S=W2S, pwx=pwx):
                            ins = None
                            for half in range(2):
                                for fc in range(nfc):
                                    ins = e.matmul(PW[pwx][:, half * 512:(half + 1) * 512], lhsT=HID[:, fc, jj * 128:(jj + 1) * 128],
                                                   rhs=W2S[:, fc, half * 512:(half + 1) * 512], start=(fc == 0), stop=(fc == nfc - 1))
                            return ins
                        op("pe", dm, reads=[bHID, bFS2], writes=[bPS[2 * pwx], bPS[2 * pwx + 1]])
                        if w == 1:
                            tt, btt = ttr.next()
                            op("dve", lambda e: e.tensor_tensor(out=tt, in0=PW[pwx][:, :], in1=BC[1], op=ALU.mult),
                               reads=[bPS[2 * pwx], bPS[2 * pwx + 1], bBC], writes=[btt])
                            if s == 0:
                                op("dve", lambda e: e.scalar_tensor_tensor(out=XS[:, j, :], in0=XS[:, j, :], scalar=ALPHA, in1=tt,
                                                                           op0=ALU.mult, op1=ALU.add),
                                   reads=[bXS[j], btt], writes=[bXS[j]])
                            else:
                                op("dve", lambda e: e.tensor_tensor(out=XS[:, j, :], in0=XS[:, j, :], in1=tt, op=ALU.add),
                                   reads=[bXS[j], btt], writes=[bXS[j]])
                        elif s == 0:
                            op("dve", lambda e: e.scalar_tensor_tensor(out=XS[:, j, :], in0=XS[:, j, :], scalar=ALPHA, in1=PW[pwx][:, :],
                                                                       op0=ALU.mult, op1=ALU.add),
                               reads=[bXS[j], bPS[2 * pwx], bPS[2 * pwx + 1]], writes=[bXS[j]])
                        else:
                            op("dve", lambda e: e.tensor_tensor(out=XS[:, j, :], in0=XS[:, j, :], in1=PW[pwx][:, :], op=ALU.add),
                               reads=[bXS[j], bPS[2 * pwx], bPS[2 * pwx + 1]], writes=[bXS[j]])
                        if s == nsl - 1:
                            pend_ln.append(j)
            if pend_ln:
                stage_ln["a"] = list(pend_ln)
                pend_ln[:] = []
            for _ in range(3):
                ln_step()
            S.barrier()
        S.emit()
    return nc


def _consts():
    f = np.float32
    c = {}
    c["k_ident"] = np.eye(128, dtype=f)
    i64 = np.arange(64)
    ang = 2.0 * np.pi * np.outer(i64, i64) / 64.0
    bdc = np.zeros((128, 128), f)
    bds = np.zeros((128, 128), f)
    for b in range(2):
        bdc[b * 64:(b + 1) * 64, b * 64:(b + 1) * 64] = np.cos(ang)
        bds[b * 64:(b + 1) * 64, b * 64:(b + 1) * 64] = np.sin(ang)
    c["k_bdc"], c["k_bds"] = bdc, bds
    t = np.arange(SEQ)
    row = (t // 64).astype(np.float64)
    col = (t % 64).astype(np.float64)
    inv = 10000.0 ** (-np.arange(8, dtype=np.float64) / 8.0)
    ar, ac = np.outer(inv, row), np.outer(inv, col)
    cosr, sinr, cosc, sinc = np.cos(ar), np.sin(ar), np.cos(ac), np.sin(ac)
    kc = np.zeros((128, SEQ), f)
    ks = np.zeros((128, SEQ), f)
    kc[64:72], kc[72:80], kc[80:88], kc[88:96] = cosr, cosr, cosc, cosc
    ks[64:72], ks[72:80], ks[80:88], ks[88:96] = -sinr, sinr, -sinc, sinc
    c["k_cos"], c["k_sin"] = kc, ks
    for L, nm in ((256, "256"), (SEQ, "L")):
        k = np.arange(L, dtype=np.int64)
        th = 2.0 * np.pi * ((np.outer(k, k) % L).astype(np.float64)) / L
        sc = 1.0 / math.sqrt(64.0 * L)
        c["k_c" + nm] = (np.cos(th) * sc).astype(f)
        c["k_s" + nm] = (-np.sin(th) * sc).astype(f)
    edge = np.ones((128, 64), f)
    for g, w in enumerate((2, 4, 8, 16)):
        hw = w // 2
        for i in range(hw):
            edge[:, g * 16 + i] = w / float(i + hw)
        for q in range(hw - 1):
            j = hw - 2 - q
            edge[:, g * 16 + 8 + q] = w / float(j + 1 + hw)
    c["k_edge"] = edge
    return c


_NC_CACHE = {}


def _prep_inputs(inp):
    f = np.float32
    A = lambda a: np.ascontiguousarray(np.asarray(a, dtype=f))
    sh = {}
    sh["w_mod"] = A(inp["w_mod"])
    sh["b_mod"] = A(inp["b_mod"])
    sh["b_modT"] = A(np.asarray(inp["b_mod"]).reshape(DEPTH, 48, 128).transpose(2, 0, 1).reshape(128, DEPTH * 48))
    w_in = np.asarray(inp["w_in"])
    sh["w_in"] = A(w_in)
    sh["w_in_krp"] = A(w_in[:, :, 384 + np.array(ROPE_PERM)])
    sh["q_normT"] = A(np.asarray(inp["q_norm"]).reshape(DEPTH, 2, 128).transpose(0, 2, 1))
    sh["kv_normT"] = A(np.asarray(inp["kv_norm"]).reshape(DEPTH, 1, 128).transpose(0, 2, 1))
    w_uq = np.asarray(inp["w_uq"])
    sh["w_uq"] = A(w_uq)
    cols = np.concatenate([h * 96 + 64 + np.array(ROPE_PERM) for h in range(8)])
    sh["w_uq_rp"] = A(w_uq[:, :, cols])
    sh["w_uk"] = A(inp["w_uk"])
    sh["w_uv"] = A(inp["w_uv"])
    sh["sgu_g"] = A(inp["sgu_ln_g"])
    sh["sgu_b"] = A(inp["sgu_ln_b"])
    sh["w_spT"] = A(np.asarray(inp["w_spatial"]).transpose(0, 1, 3, 2))
    sh["b_sp"] = A(np.asarray(inp["b_spatial"]).reshape(DEPTH, 512))
    sh["w_pool"] = A(inp["w_pool"])
    sh["pool_scT"] = A(np.asarray(inp["pool_scale"]).reshape(DEPTH, 2, 128).transpose(0, 2, 1))
    sh["w_fou"] = A(inp["w_fourier"])
    sh["w_out"] = A(inp["w_out"])
    sh["ln1_g"], sh["ln1_b"] = A(inp["ln1_g"]), A(inp["ln1_b"])
    sh["w_f1"], sh["w_f3"], sh["w_f2"] = A(inp["w_ffn1"]), A(inp["w_ffn3"]), A(inp["w_ffn2"])
    sh["ln2_g"], sh["ln2_b"] = A(inp["ln2_g"]), A(inp["ln2_b"])
    sh.update(_consts())
    x = np.asarray(inp["x"], dtype=f)
    ctx = np.asarray(inp["ctx"], dtype=f)
    c = np.asarray(inp["c"], dtype=f)
    cc = np.asarray(inp["c_ctx"], dtype=f)
    maps = []
    for b in range(8):
        m = dict(sh)
        m["x_in"] = np.ascontiguousarray(np.concatenate([ctx[b], x[b]], axis=0))
        cv = np.stack([c[b], cc], axis=0).reshape(2, 8, 128).transpose(2, 1, 0)
        m["cvec"] = np.ascontiguousarray(cv)
        maps.append(m)
    return maps


def kernel(**inputs):
    maps = _prep_inputs(inputs)
    if "nc" not in _NC_CACHE:
        _NC_CACHE["nc"] = build()
    res = run_bass_kernel_spmd(_NC_CACHE["nc"], maps, core_ids=list(range(8)))
    out = np.stack([np.asarray(r["out"], dtype=np.float32) for r in res.results], axis=0)
    return out
```
